# Optimizing a Trainium2 kernel written in Bass

```python
import jax, jax.numpy as jnp
from jax import lax
import numpy as np

D_MODEL = 1024
BATCH = 16
SEQ = 2048
DEPTH = 2

N_MIXERS = 2
N_MLA_LAYERS = (DEPTH + 1) // 2
N_FOX_LAYERS = DEPTH // 2

MLA_HEADS = 8
MLA_NOPE_DIM = 128
MLA_ROPE_DIM = 64
MLA_V_DIM = 128
MLA_Q_RANK = 256
MLA_KV_RANK = 256
ROPE_THETA = 10000.0

FOX_HEADS = 16
FOX_HEAD_DIM = D_MODEL // FOX_HEADS

D_FF = -(-8 * D_MODEL // (3 * 256)) * 256

Q_BLOCK = 128
DEEPNORM_ALPHA = (2.0 * DEPTH) ** 0.25
DEEPNORM_BETA = (8.0 * DEPTH) ** -0.25
NORM_EPS = 1e-5
MLA_IN_DIM = MLA_Q_RANK + MLA_KV_RANK + MLA_ROPE_DIM
FOX_IN_DIM = 3 * D_MODEL + FOX_HEADS

kernel_name = "hybrid_mla_fox_deepnorm_adaln"


def rms_norm(x, g):
    xf = x.astype(jnp.float32)
    y = xf * lax.rsqrt(jnp.mean(xf * xf, axis=-1, keepdims=True) + NORM_EPS)
    return (y * g.astype(jnp.float32)).astype(x.dtype)


def layer_norm(x, g, b):
    xf = x.astype(jnp.float32)
    mu = jnp.mean(xf, axis=-1, keepdims=True)
    var = jnp.mean(jnp.square(xf - mu), axis=-1, keepdims=True)
    y = (xf - mu) * lax.rsqrt(var + NORM_EPS)
    return (y * g.astype(jnp.float32) + b.astype(jnp.float32)).astype(x.dtype)


def rotary_angles(positions, dim):
    half = dim // 2
    inv_freq = ROPE_THETA ** (-jnp.arange(half, dtype=jnp.float32) / half)
    ang = positions.astype(jnp.float32)[..., None] * inv_freq
    return jnp.cos(ang), jnp.sin(ang)


def apply_rotary(x, cos, sin):
    half = x.shape[-1] // 2
    x1, x2 = x[..., :half], x[..., half:]
    cos = cos.astype(x.dtype)
    sin = sin.astype(x.dtype)
    return jnp.concatenate([x1 * cos - x2 * sin, x2 * cos + x1 * sin], axis=-1)


def causal_block_attention(logits_fn, v):
    b, h, s, dv = v.shape
    key_pos = jnp.arange(s)

    def one_block(blk):
        q_start = blk * Q_BLOCK
        logits = logits_fn(q_start)
        q_pos = q_start + jnp.arange(Q_BLOCK)
        causal = q_pos[:, None] >= key_pos[None, :]
        probs = jax.nn.softmax(jnp.where(causal, logits, -jnp.inf), axis=-1)
        return jnp.einsum('bhqs,bhsd->bhqd', probs.astype(v.dtype), v)

    out = lax.map(one_block, jnp.arange(s // Q_BLOCK))
    return out.transpose(1, 0, 3, 2, 4).reshape(b, s, h * dv)


def mla_mixer(u, cos, sin, w_in, g_q, w_uq, g_kv, w_uk, w_uv, w_o):
    b, s, _ = u.shape
    h_in = u @ w_in
    c_q = rms_norm(h_in[..., :MLA_Q_RANK], g_q)
    c_kv = rms_norm(h_in[..., MLA_Q_RANK:MLA_Q_RANK + MLA_KV_RANK], g_kv)
    k_rope = apply_rotary(h_in[..., MLA_Q_RANK + MLA_KV_RANK:], cos, sin)

    q = (c_q @ w_uq).reshape(b, s, MLA_HEADS, MLA_NOPE_DIM + MLA_ROPE_DIM)
    q_nope = q[..., :MLA_NOPE_DIM].transpose(0, 2, 1, 3)
    q_rope = apply_rotary(q[..., MLA_NOPE_DIM:], cos[:, :, None, :], sin[:, :, None, :])
    q_rope = q_rope.transpose(0, 2, 1, 3)
    k_nope = (c_kv @ w_uk).reshape(b, s, MLA_HEADS, MLA_NOPE_DIM).transpose(0, 2, 1, 3)
    v = (c_kv @ w_uv).reshape(b, s, MLA_HEADS, MLA_V_DIM).transpose(0, 2, 1, 3)
    scale = (MLA_NOPE_DIM + MLA_ROPE_DIM) ** -0.5

    def logits_fn(q_start):
        qn = lax.dynamic_slice_in_dim(q_nope, q_start, Q_BLOCK, axis=2)
        qr = lax.dynamic_slice_in_dim(q_rope, q_start, Q_BLOCK, axis=2)
        sc = (jnp.einsum('bhqd,bhsd->bhqs', qn, k_nope)
              + jnp.einsum('bhqr,bsr->bhqs', qr, k_rope))
        return sc.astype(jnp.float32) * scale

    return causal_block_attention(logits_fn, v) @ w_o


def fox_mixer(u, w_in, b_f, w_o):
    b, s, d = u.shape
    h_in = u @ w_in
    def heads(t):
        return t.reshape(b, s, FOX_HEADS, FOX_HEAD_DIM).transpose(0, 2, 1, 3)
    q = heads(h_in[..., :d])
    k = heads(h_in[..., d:2 * d])
    v = heads(h_in[..., 2 * d:3 * d])
    log_f = jax.nn.log_sigmoid(h_in[..., 3 * d:].astype(jnp.float32) + b_f.astype(jnp.float32))
    cum_log_f = lax.cumsum(log_f, axis=1).transpose(0, 2, 1)
    scale = FOX_HEAD_DIM ** -0.5

    def logits_fn(q_start):
        qb = lax.dynamic_slice_in_dim(q, q_start, Q_BLOCK, axis=2)
        fq = lax.dynamic_slice_in_dim(cum_log_f, q_start, Q_BLOCK, axis=2)
        sc = jnp.einsum('bhqd,bhsd->bhqs', qb, k).astype(jnp.float32) * scale
        return sc + fq[..., :, None] - cum_log_f[:, :, None, :]

    return causal_block_attention(logits_fn, v) @ w_o


def swiglu(u, w_gate, w_up, w_down):
    return (jax.nn.silu(u @ w_gate) * (u @ w_up)) @ w_down


def modulate(x, shift, scale):
    return x * (1.0 + scale[:, None, :]) + shift[:, None, :]


def setup_inputs(seed: int = 0) -> dict:
    key = jax.random.key(seed)
    ks = iter(jax.random.split(key, 40))
    f32 = jnp.float32
    def nrm(shape, std):
        return jax.random.normal(next(ks), shape, f32) * std
    D, H, Hf = D_MODEL, MLA_HEADS, FOX_HEADS
    beta = DEEPNORM_BETA
    nm, nf = N_MLA_LAYERS, N_FOX_LAYERS

    x = jax.random.normal(next(ks), (BATCH, SEQ, D), f32)
    c = jax.random.normal(next(ks), (BATCH, D), f32)
    positions = (jnp.arange(SEQ, dtype=jnp.int32)[None, :]
                 + jax.random.randint(next(ks), (BATCH, 1), 0, 128, dtype=jnp.int32))

    mla_w_in = nrm((nm, D, MLA_IN_DIM), D ** -0.5)
    mla_g_q = 1.0 + nrm((nm, MLA_Q_RANK), 0.02)
    mla_w_uq = nrm((nm, MLA_Q_RANK, H * (MLA_NOPE_DIM + MLA_ROPE_DIM)), MLA_Q_RANK ** -0.5)
    mla_g_kv = 1.0 + nrm((nm, MLA_KV_RANK), 0.02)
    mla_w_uk = nrm((nm, MLA_KV_RANK, H * MLA_NOPE_DIM), MLA_KV_RANK ** -0.5)
    mla_w_uv = nrm((nm, MLA_KV_RANK, H * MLA_V_DIM), beta * MLA_KV_RANK ** -0.5)
    mla_w_o = nrm((nm, H * MLA_V_DIM, D), beta * (H * MLA_V_DIM) ** -0.5)

    fox_w_in = jnp.concatenate([
        nrm((nf, D, 2 * D), D ** -0.5),
        nrm((nf, D, D), beta * D ** -0.5),
        nrm((nf, D, Hf), D ** -0.5),
    ], axis=-1)
    fox_b_f = 2.0 + nrm((nf, Hf), 0.5)
    fox_w_o = nrm((nf, D, D), beta * D ** -0.5)

    ada_w = nrm((DEPTH, D, 6 * D), 0.1 * D ** -0.5)
    ada_b = nrm((DEPTH, 6 * D), 0.02)

    ffn_w_gate = nrm((DEPTH, D, D_FF), beta * D ** -0.5)
    ffn_w_up = nrm((DEPTH, D, D_FF), beta * D ** -0.5)
    ffn_w_down = nrm((DEPTH, D_FF, D), beta * D_FF ** -0.5)

    ln_g = 1.0 + nrm((DEPTH, 2, D), 0.02)
    ln_b = nrm((DEPTH, 2, D), 0.02)

    return {"x": x, "c": c, "positions": positions,
            "mla_w_in": mla_w_in, "mla_g_q": mla_g_q, "mla_w_uq": mla_w_uq,
            "mla_g_kv": mla_g_kv, "mla_w_uk": mla_w_uk, "mla_w_uv": mla_w_uv, "mla_w_o": mla_w_o,
            "fox_w_in": fox_w_in, "fox_b_f": fox_b_f, "fox_w_o": fox_w_o,
            "ada_w": ada_w, "ada_b": ada_b,
            "ffn_w_gate": ffn_w_gate, "ffn_w_up": ffn_w_up, "ffn_w_down": ffn_w_down,
            "ln_g": ln_g, "ln_b": ln_b}


def reference(x, c, positions, mla_w_in, mla_g_q, mla_w_uq, mla_g_kv, mla_w_uk, mla_w_uv, mla_w_o,
              fox_w_in, fox_b_f, fox_w_o, ada_w, ada_b, ffn_w_gate, ffn_w_up, ffn_w_down,
              ln_g, ln_b):
    d = D_MODEL
    cos, sin = rotary_angles(positions, MLA_ROPE_DIM)
    c_act = jax.nn.silu(c)
    for i in range(DEPTH):
        mod = c_act @ ada_w[i] + ada_b[i]
        sh_a, sc_a, gt_a = mod[:, :d], mod[:, d:2 * d], mod[:, 2 * d:3 * d]
        sh_f, sc_f, gt_f = mod[:, 3 * d:4 * d], mod[:, 4 * d:5 * d], mod[:, 5 * d:]

        u = modulate(x, sh_a, sc_a)
        j = i // N_MIXERS
        if i % N_MIXERS == 0:
            y = mla_mixer(u, cos, sin, mla_w_in[j], mla_g_q[j], mla_w_uq[j], mla_g_kv[j],
                          mla_w_uk[j], mla_w_uv[j], mla_w_o[j])
        else:
            y = fox_mixer(u, fox_w_in[j], fox_b_f[j], fox_w_o[j])
        x = layer_norm(DEEPNORM_ALPHA * x + (1.0 + gt_a[:, None, :]) * y, ln_g[i, 0], ln_b[i, 0])

        u = modulate(x, sh_f, sc_f)
        y = swiglu(u, ffn_w_gate[i], ffn_w_up[i], ffn_w_down[i])
        x = layer_norm(DEEPNORM_ALPHA * x + (1.0 + gt_f[:, None, :]) * y, ln_g[i, 1], ln_b[i, 1])
    return x
```

```python
import math
from bisect import bisect_left
from contextlib import ExitStack

import numpy as np
import concourse.bass as bass
import concourse.mybir as mybir
from concourse.bass_utils import run_bass_kernel_spmd

F32 = mybir.dt.float32
BF16 = mybir.dt.bfloat16
I32 = mybir.dt.int32
ALU = mybir.AluOpType
AF = mybir.ActivationFunctionType

D = 1024
S = 2048
NSEQ = 2
NB = 16
DFF = 2816
NCORES = 8
ALPHA = (2.0 * 2) ** 0.25
EPS = 1e-5
MLA_SCALE = 192.0 ** -0.5
TWO_PI = 2.0 * math.pi
CW1 = 6.28125
CW2 = TWO_PI - 6.28125

GRAN = 256
SB_WORDS = 52500
NDMA_SEMS = 12


class Item:
    __slots__ = ("fn", "waits", "signal", "dma")

    def __init__(self, fn, waits, dma=None):
        self.fn = fn
        self.waits = waits
        self.signal = False
        self.dma = dma


class Eng:
    def __init__(self, name):
        self.name = name
        self.key = "E:" + name
        self.items = []
        self.waited = {}
        self.dma_n = 0
        self.dma_vals = [0] * NDMA_SEMS
        self.needed = set()
        self.ticket_of = None


class DmaSrc:
    __slots__ = ("key", "val")

    def __init__(self, key, val):
        self.key = key
        self.val = val


class V:
    __slots__ = ("ap", "sp", "g0", "g1")

    def __init__(self, ap, sp, g0, g1):
        self.ap = ap
        self.sp = sp
        self.g0 = g0
        self.g1 = g1

    def w(self, ap):
        return V(ap, self.sp, self.g0, self.g1)


class Prog:
    def __init__(self):
        self.eng = {n: Eng(n) for n in ("pe", "act", "dve", "pool", "sp")}
        self.spaces = {}

    def space(self, name, ngran):
        self.spaces[name] = ([None] * ngran, [None] * ngran)

    @staticmethod
    def _upd(need, src, seq):
        o = need.get(src)
        if o is None or o < seq:
            need[src] = seq

    def _deps(self, E, reads, writes):
        need = {}
        upd = self._upd
        for r in reads:
            lastw, _ = self.spaces[r.sp]
            for g in range(r.g0, r.g1):
                w = lastw[g]
                if w is not None:
                    upd(need, w[0], w[1])
        for r in writes:
            lastw, rd = self.spaces[r.sp]
            for g in range(r.g0, r.g1):
                w = lastw[g]
                if w is not None:
                    upd(need, w[0], w[1])
                rr = rd[g]
                if rr:
                    for s_, q in rr.items():
                        upd(need, s_, q)
        return need

    def _waits(self, E, need, self_ok):
        waits = []
        for src, seq in need.items():
            if src is E and self_ok:
                continue
            if isinstance(src, DmaSrc):
                if E.waited.get(src.key, 0) >= src.val:
                    continue
                E.waited[src.key] = src.val
                waits.append((src.key, src.val))
            else:
                if E.waited.get(src.key, -1) >= seq:
                    continue
                E.waited[src.key] = seq
                src.needed.add(seq)
                waits.append((src, seq))
        return waits

    def _mark(self, src, seq, reads, writes):
        for r in reads:
            _, rd = self.spaces[r.sp]
            for g in range(r.g0, r.g1):
                d = rd[g]
                if d is None:
                    rd[g] = {src: seq}
                else:
                    d[src] = seq
        for r in writes:
            lastw, rd = self.spaces[r.sp]
            t = (src, seq)
            for g in range(r.g0, r.g1):
                lastw[g] = t
                rd[g] = None

    def op(self, eng, fn, reads=(), writes=()):
        E = self.eng[eng]
        ps_reads = [r for r in reads if r.sp == "ps"]
        if ps_reads:
            writes = list(writes) + ps_reads
        need = self._deps(E, reads, writes)
        waits = self._waits(E, need, self_ok=(eng == "pe"))
        seq = len(E.items)
        E.items.append(Item(fn, waits))
        if ps_reads:
            reads = [r for r in reads if r.sp != "ps"]
        self._mark(E, seq, reads, writes)

    def dma(self, q, fn, reads=(), writes=()):
        E = self.eng[q]
        slot = E.dma_n % NDMA_SEMS
        E.dma_n += 1
        key = "D:%s:%d" % (q, slot)
        prev = E.dma_vals[slot]
        val = prev + 16
        E.dma_vals[slot] = val
        need = self._deps(None, reads, writes)
        waits = self._waits(E, need, self_ok=False)
        if prev > 0 and E.waited.get(key, 0) < prev:
            E.waited[key] = prev
            waits.append((key, prev))
        E.items.append(Item(fn, waits, dma=(key, val)))
        self._mark(DmaSrc(key, val), 0, reads, writes)

    def finish(self):
        sp = self.eng["sp"]
        waits = []
        for q in ("sp", "pool", "act"):
            E = self.eng[q]
            for slot in range(NDMA_SEMS):
                v = E.dma_vals[slot]
                key = "D:%s:%d" % (q, slot)
                if v > 0 and sp.waited.get(key, 0) < v:
                    sp.waited[key] = v
                    waits.append((key, v))
        sp.items.append(Item(None, waits))
        self.resolve()

    def resolve(self):
        for E in self.eng.values():
            order = sorted(E.needed)
            E.ticket_of = {seq: i + 1 for i, seq in enumerate(order)}
            for seq in order:
                assert E.items[seq].dma is None
                E.items[seq].signal = True
            E.sig = order
        for E in self.eng.values():
            for it in E.items:
                it.waits = [(w[0].key, w[0].ticket_of[w[1]]) if isinstance(w[0], Eng) else w for w in it.waits]


class StopBuild(Exception):
    pass


class Builder:
    def __init__(self, nc, arena, psum, dram, stop_after=None, nseq=NSEQ):
        self.nc = nc
        self.arena = arena
        self.psum = psum
        self.dr = dram
        self.P = Prog()
        self.P.space("sb", (SB_WORDS * 4 + GRAN - 1) // GRAN)
        self.P.space("ps", 8)
        self.P.space("modrows", 4)
        self.P.space("out", NSEQ * NB)
        self.stop_after = stop_after
        self.nseq = nseq
        self.rr = 0
        self.tmp_rr = 0
        self.layout()

    def f32(self, w0, n, p0=0, p1=128):
        ap = self.arena[p0:p1, w0:w0 + n]
        return V(ap, "sb", (w0 * 4) // GRAN, (4 * (w0 + n) + GRAN - 1) // GRAN)

    def i32(self, w0, n, p0=0, p1=128):
        v = self.f32(w0, n, p0, p1)
        return v.w(v.ap.bitcast(I32))

    def bf(self, w0, e0, n, p0=0, p1=128):
        assert e0 % 2 == 0 and n % 2 == 0
        a = w0 + e0 // 2
        b = a + n // 2
        ap = self.arena[p0:p1, a:b].bitcast(BF16)
        return V(ap, "sb", (a * 4) // GRAN, (4 * b + GRAN - 1) // GRAN)

    def ps(self, bank, c0=0, n=512, p0=0, p1=128):
        a = bank * 512 + c0
        assert c0 + n <= 512
        ap = self.psum[p0:p1, a:a + n]
        return V(ap, "ps", bank, bank + 1)

    def psb(self, bank, e0, n, p0=0, p1=128):
        a = bank * 512 + e0 // 2
        b = a + n // 2
        assert b <= (bank + 1) * 512
        ap = self.psum[p0:p1, a:b].bitcast(BF16)
        return V(ap, "ps", bank, bank + 1)

    def nbank(self):
        b = self.rr
        self.rr = (self.rr + 1) % 8
        return b

    def tmp(self):
        i = self.tmp_rr
        self.tmp_rr = (self.tmp_rr + 1) % 6
        return self.W_TMP + 512 * i

    def layout(self):
        self.W_X = 0
        self.W_UT = 16384
        c = 24576
        self.W_IDB = c
        self.W_ONB = c + 64
        self.W_NIB = c + 128
        self.W_UTB = c + 1600
        self.W_STAT2 = c + 1664
        self.W_IDF = c + 192
        self.W_NEGH = c + 320
        self.W_MODT = c + 832
        self.W_GQ = c + 1024
        self.W_GKV = c + 1088
        self.W_INVF = c + 1152
        self.W_BF = c + 1216
        self.W_STAT = c + 1280
        self.W_CACT = c + 1408
        self.W_CT = c + 1472
        self.W_WF = c + 1536
        self.W_TMP = 26624
        self.W_LNV = 29696
        self.W_XB = 32768
        self.W_TAB = self.W_LNV + 1024
        self.W_PT = 34816
        ph = 35840
        self.W_PH = ph
        self.W_CQ = ph
        self.W_CKV = ph + 2048
        self.W_KR = ph + 4096
        self.W_QN = ph + 5120
        self.W_QR = ph + 6144
        self.W_KN = ph + 7168
        self.W_VH = ph + 8192
        self.W_WIN_M = ph + 5120
        self.W_WUQ = ph + 9216
        self.W_WUK = ph + 11264
        self.W_WUV = ph + 12288
        self.W_WO_M = ph + 13312
        self.W_AO = ph
        self.W_QA = ph + 2048
        self.W_KA = ph + 4096
        self.W_VG = ph + 6144
        self.W_WIN_F = ph + 10240
        self.W_WO_F = ph + 13312
        self.W_GC = ph + 14336
        self.W_FSP = ph + 15360
        self.W_GTOK = ph + 16384
        self.W_FT = self.W_PT
        self.W_HT = ph
        self.W_WG = ph + 8192
        self.W_WU = ph + 10240
        self.W_WD = ph + 12288
        self.W_ADAW = ph
        self.W_MODROW = ph + 4096
        self.W_ADAB = ph + 10240
        assert ph + 16640 <= SB_WORDS

    def mm(self, out, lhsT, rhs, start, stop):
        self.P.op("pe", lambda e, o=out.ap, l=lhsT.ap, r=rhs.ap, a=start, b=stop:
                  e.matmul(o, l, r, start=a, stop=b), reads=[lhsT, rhs], writes=[out])

    def tr(self, out, in_, ident):
        self.P.op("pe", lambda e, o=out.ap, i=in_.ap, d=ident.ap: e.transpose(o, i, d),
                  reads=[in_, ident], writes=[out])

    def act(self, out, in_, func, scale=1.0, bias=0.0):
        reads = [in_]
        sc = scale
        bi = bias
        if isinstance(scale, V):
            reads.append(scale)
            sc = scale.ap
        if isinstance(bias, V):
            reads.append(bias)
            bi = bias.ap
        self.P.op("act", lambda e, o=out.ap, i=in_.ap, f=func, s=sc, b=bi:
                  e.activation(out=o, in_=i, func=f, scale=s, bias=b), reads=reads, writes=[out])

    def tt(self, eng, out, in0, in1, op):
        self.P.op(eng, lambda e, o=out.ap, a=in0.ap, b=in1.ap, p=op:
                  e.tensor_tensor(out=o, in0=a, in1=b, op=p), reads=[in0, in1], writes=[out])

    def ts(self, eng, out, in0, s1, op0, s2=None, op1=None):
        reads = [in0]
        a1 = s1
        a2 = s2
        if isinstance(s1, V):
            reads.append(s1)
            a1 = s1.ap
        if isinstance(s2, V):
            reads.append(s2)
            a2 = s2.ap
        if op1 is None:
            self.P.op(eng, lambda e, o=out.ap, a=in0.ap, x=a1, p0=op0:
                      e.tensor_scalar(out=o, in0=a, scalar1=x, scalar2=None, op0=p0),
                      reads=reads, writes=[out])
        else:
            self.P.op(eng, lambda e, o=out.ap, a=in0.ap, x=a1, y=a2, p0=op0, p1=op1:
                      e.tensor_scalar(out=o, in0=a, scalar1=x, scalar2=y, op0=p0, op1=p1),
                      reads=reads, writes=[out])

    def stt(self, out, in0, scalar, in1, op0, op1):
        reads = [in0, in1]
        sc = scalar
        if isinstance(scalar, V):
            reads.append(scalar)
            sc = scalar.ap
        self.P.op("dve", lambda e, o=out.ap, a=in0.ap, s=sc, b=in1.ap, p0=op0, p1=op1:
                  e.scalar_tensor_tensor(out=o, in0=a, scalar=s, in1=b, op0=p0, op1=p1),
                  reads=reads, writes=[out])

    def copy(self, eng, out, in_):
        if eng == "act":
            self.act(out, in_, AF.Copy)
        else:
            self.P.op(eng, lambda e, o=out.ap, i=in_.ap: e.tensor_copy(out=o, in_=i),
                      reads=[in_], writes=[out])

    def memset(self, eng, out, val):
        self.P.op(eng, lambda e, o=out.ap, v=val: e.memset(o, v), writes=[out])

    def dma(self, q, out_ap, in_ap, reads, writes, nc_ok=False):
        if nc_ok:
            self.P.dma(q, lambda e, o=out_ap, i=in_ap: e.dma_start(out=o, in_=i, allow_slow_non_contiguous=True),
                       reads=reads, writes=writes)
        else:
            self.P.dma(q, lambda e, o=out_ap, i=in_ap: e.dma_start(out=o, in_=i), reads=reads, writes=writes)

    def wload(self, dst, src_ap):
        self.dma("pool", dst.ap, src_ap, reads=[], writes=[dst])

    def evac_engine(self):
        self._ev = getattr(self, "_ev", 0) + 1
        return "act" if self._ev % 2 else "dve"

    def X(self, blk, c0=0, n=1024):
        return self.f32(self.W_X + blk * 1024 + c0, n)

    def UT(self, k, t0, n, p0=0, p1=128):
        return self.bf(self.W_UT, k * 2048 + t0, n, p0, p1)

    def XB(self, j, c0=0, n=1024):
        return self.bf(self.W_XB + 512 * j, c0, n)

    def LNV(self, i, c0=0, n=1024):
        return self.f32(self.W_LNV + 1024 * i + c0, n)

    def PT(self, i, c0=0, n=512):
        return self.bf(self.W_PT + 256 * i, c0, n)

    def modcol(self, layer, vec, ch, seq):
        return self.f32(self.W_MODT + ((layer * 48) + vec * 8 + ch) * 2 + seq, 1)

    def IDB(self, n=128):
        return self.bf(self.W_IDB, 0, 128, 0, n) if n == 128 else V(
            self.arena[0:n, self.W_IDB:self.W_IDB + 64].bitcast(BF16)[:, 0:n], "sb",
            (self.W_IDB * 4) // GRAN, (self.W_IDB * 4) // GRAN + 1)

    def IDF(self, n):
        v = self.f32(self.W_IDF, 128, 0, n)
        return v.w(v.ap[:, 0:n])

    def prologue(self):
        dr = self.dr
        st0 = self.f32(self.tmp(), 512)
        self.dma("sp", st0.ap[:, 0:512], dr["cbf"][:, 0:512], reads=[], writes=[st0])
        for i, w in enumerate((self.W_IDB, self.W_ONB, self.W_NIB, self.W_UTB)):
            dst = self.bf(w, 0, 128)
            self.copy("dve", dst, st0.w(st0.ap[:, i * 128:(i + 1) * 128]))
        idf = self.f32(self.W_IDF, 128)
        self.dma("sp", idf.ap, dr["cbf"][:, 0:128], reads=[], writes=[idf])
        negh = self.f32(self.W_NEGH, 512)
        self.memset("pool", negh, -0.5)
        small = self.f32(self.W_GQ, 64 * 4 + 64)
        gq = self.f32(self.W_GQ, 2)
        self.dma("sp", gq.ap, dr["gq"][:, :], reads=[], writes=[gq])
        gkv = self.f32(self.W_GKV, 2)
        self.dma("sp", gkv.ap, dr["gkv"][:, :], reads=[], writes=[gkv])
        invf = self.f32(self.W_INVF, 2, 0, 64)
        self.dma("sp", invf.ap, dr["invf"][:, :], reads=[], writes=[invf])
        bfc = self.f32(self.W_BF, 1, 0, 16)
        self.dma("sp", bfc.ap, dr["fox_bf"][:, :], reads=[], writes=[bfc])
        nbf = self.f32(self.W_BF + 1, 1, 0, 16)
        self.ts("dve", nbf, bfc, -1.0, ALU.mult)
        wf = self.bf(self.W_WF, 0, 128)
        self.wload(wf.w(wf.ap.rearrange("p (k n) -> p k n", k=8)),
                   dr["fox_w_in"][:, 3072:3088].rearrange("(k p) n -> p k n", p=128))
        ct = self.f32(self.W_CT, 16)
        self.dma("sp", ct.ap, dr["cT"][:, :], reads=[], writes=[ct])
        cact = self.bf(self.W_CACT, 0, 16)
        self.act(cact, ct, AF.Silu)
        for layer in range(2):
            adab = self.f32(self.W_ADAB, 6144, 0, 2)
            self.dma("sp", adab.ap, dr["ada_b"][layer:layer + 1, :].to_broadcast([2, 6144]),
                     reads=[], writes=[adab])
            for cg in range(12):
                slot = cg % 2
                stg = self.f32(self.W_UT + 4096 * slot, 4096)
                self.dma("sp", stg.ap.rearrange("p (k n) -> p k n", k=8),
                         dr["ada_w"][layer, :, cg * 512:(cg + 1) * 512].rearrange("(k p) n -> p k n", p=128),
                         reads=[], writes=[stg])
                self.copy("act", self.bf(self.W_ADAW + 2048 * slot, 0, 2048), self.f32(self.W_UT + 4096 * slot, 2048))
                self.copy("dve", self.bf(self.W_ADAW + 2048 * slot, 2048, 2048), self.f32(self.W_UT + 4096 * slot + 2048, 2048))
                b = self.nbank()
                pso = self.ps(b, 0, 512, 0, 2)
                for k in range(8):
                    self.mm(pso, self.bf(self.W_CACT, 2 * k, 2),
                            self.bf(self.W_ADAW + 2048 * slot, k * 512, 512), k == 0, k == 7)
                mr = self.f32(self.W_MODROW + cg * 512, 512, 0, 2)
                self.tt("dve", mr, pso, self.f32(self.W_ADAB + cg * 512, 512, 0, 2), ALU.add)
            for v in (1, 2, 4, 5):
                sl = self.f32(self.W_MODROW + v * 1024, 1024, 0, 2)
                self.ts("dve", sl, sl, 1.0, ALU.add)
            b = self.nbank()
            for c in range(48):
                self.tr(self.ps(b, 2 * c, 2), self.f32(self.W_MODROW + c * 128, 128, 0, 2), self.IDF(2))
            self.copy("dve", self.f32(self.W_MODT + layer * 96, 96), self.ps(b, 0, 96))
            mrow = self.f32(self.W_MODROW, 6144, 0, 2)
            self.dma("sp", dr["modrows"][2 * layer:2 * layer + 2, :], mrow.ap,
                     reads=[mrow], writes=[V(None, "modrows", 2 * layer, 2 * layer + 2)])

    def load_x_block(self, seq, blk):
        xv = self.X(blk)
        r0 = seq * S + blk * 128
        self.dma("sp", xv.ap, self.dr["x"][r0:r0 + 128, :], reads=[], writes=[xv])

    def load_x(self, seq):
        for blk in range(NB):
            self.load_x_block(seq, blk)

    def seq_front(self, seq):
        self.load_x(seq)
        self.mla_load_weights()
        for st in range(4):
            for j in range(4):
                self.cast_xb(4 * st + j, j)
            self.ut_subtile(0, 0, 1, seq, st)
            self.mla_latents_rms(st)

    def cast_xb(self, blk, j):
        self.act(self.XB(j), self.X(blk), AF.Copy)

    def ut_subtile(self, layer, vec_sh, vec_sc, seq, st, act_only=False):
        for ch in range(8):
            b = self.nbank()
            for j in range(4):
                self.tr(self.psb(b, j * 128, 128), self.XB(j, ch * 128, 128), self.IDB())
            dst = self.UT(ch, st * 512, 512)
            src = self.psb(b, 0, 512)
            sc = self.modcol(layer, vec_sc, ch, seq)
            sh = self.modcol(layer, vec_sh, ch, seq)
            if ch % 2 == 0 or act_only:
                self.act(dst, src, AF.Identity, scale=sc, bias=sh)
            else:
                self.ts("dve", dst, src, sc, ALU.mult, sh, ALU.add)

    def build_ut_from_x(self, layer, vec_sh, vec_sc, seq):
        for st in range(4):
            for j in range(4):
                self.cast_xb(4 * st + j, j)
            self.ut_subtile(layer, vec_sh, vec_sc, seq, st)

    def load_lnv(self, layer, which, seq, gate_only=False, ln_only=False):
        dr = self.dr
        gvec = 2 if which == 0 else 5
        if not ln_only:
            g = self.LNV(0)
            row = 2 * layer + seq
            self.dma("sp", g.ap, dr["modrows"][row:row + 1, gvec * 1024:(gvec + 1) * 1024].to_broadcast([128, 1024]),
                     reads=[V(None, "modrows", row, row + 1)], writes=[g])
        if not gate_only:
            ga = self.LNV(1)
            self.dma("sp", ga.ap, dr["ln_g"][layer, which:which + 1, :].to_broadcast([128, 1024]), reads=[], writes=[ga])
            be = self.LNV(2)
            self.dma("sp", be.ap, dr["ln_b"][layer, which:which + 1, :].to_broadcast([128, 1024]), reads=[], writes=[be])

    def resid(self, first, blk, half, psv):
        t = self.f32(self.tmp(), 512)
        self.tt("dve", t, psv, self.LNV(0, half * 512, 512), ALU.mult)
        xv = self.X(blk, half * 512, 512)
        if first:
            self.stt(xv, xv, ALPHA, t, ALU.mult, ALU.add)
        else:
            self.tt("dve", xv, xv, t, ALU.add)

    def ln_stats(self, blk):
        sb = self.W_STAT2 + (blk % 2) * 128
        st6 = self.f32(sb, 12)
        for hf in range(2):
            o = self.f32(sb + 6 * hf, 6)
            xi = self.X(blk, hf * 512, 512)
            self.P.op("dve", lambda e, o=o.ap, i=xi.ap: e.bn_stats(out=o, in_=i), reads=[xi], writes=[o])
        mv = self.f32(sb + 12, 2)
        self.P.op("dve", lambda e, o=mv.ap, i=st6.ap: e.bn_aggr(out=o, in_=i), reads=[st6], writes=[mv])
        ve = self.f32(sb + 64, 1)
        rstd = self.f32(sb + 65, 1)
        self.ts("pool", ve, self.f32(sb + 13, 1), EPS, ALU.add)
        self.tt("pool", rstd, ve, self.f32(self.W_NEGH, 1), ALU.pow)
        xv = self.X(blk)
        self.stt(xv, xv, self.f32(sb + 12, 1), self.LNV(1), ALU.subtract, ALU.mult)

    def ln_finish(self, blk, final_seq=None):
        sb = self.W_STAT2 + (blk % 2) * 128
        xv = self.X(blk)
        self.stt(xv, xv, self.f32(sb + 65, 1), self.LNV(2), ALU.mult, ALU.add)
        if final_seq is not None:
            r0 = final_seq * S + blk * 128
            self.dma("sp", self.dr["out"][r0:r0 + 128, :], xv.ap, reads=[xv],
                     writes=[V(None, "out", final_seq * NB + blk, final_seq * NB + blk + 1)])

    def ln_pipe(self, layer, which, seq, next_mod=None, final=False, hook=None, next_seq=None):
        pending = []
        if final and next_seq is not None:
            next_mod = (0, 0, 1)
            hook = lambda st: [lambda: self.rms_latent(st, (0, 1), self.W_WIN_M, self.W_GQ, self.W_CQ),
                               lambda: self.rms_latent(st, (2, 3), self.W_WIN_M, self.W_GKV, self.W_CKV)]
        ut_seq = seq if next_seq is None else next_seq

        def step(t):
            if t < NB:
                self.ln_stats(t)
                self.ln_finish(t, final_seq=(seq if final else None))
                if final and next_seq is not None:
                    self.load_x_block(next_seq, t)
            b = t - 1
            if 0 <= b < NB and (not final or next_seq is not None):
                self.cast_xb(b, b % 4)
                if b % 4 == 3:
                    nl, vsh, vsc = next_mod
                    self.ut_subtile(nl, vsh, vsc, ut_seq, b // 4, act_only=True)
                    if hook is not None:
                        pending.extend(hook(b // 4))
            if pending:
                pending.pop(0)()

        def flush():
            step(NB)
            step(NB + 1)
            for fn in pending:
                fn()
            del pending[:]

        return step, flush

    def ln_stage(self, layer, which, seq, **kw):
        step, flush = self.ln_pipe(layer, which, seq, **kw)
        for t in range(NB):
            step(t)
        flush()

    def range_wrap(self, out, in_, shift, scratch):
        if shift != 0.0:
            self.ts("dve", out, in_, shift, ALU.add)
            src = out
        else:
            src = in_
        self.ts("dve", scratch, src, math.pi, ALU.is_gt)
        self.stt(out, scratch, -TWO_PI, src, ALU.mult, ALU.add)

    def rope_tables(self, seq):
        dr = self.dr
        posi = self.i32(self.W_TMP, 2048, 0, 64)
        self.dma("sp", posi.ap, dr["pos"][seq:seq + 1, :].to_broadcast([64, 2048]), reads=[], writes=[posi])
        cosv = self.f32(self.W_TAB, 2048, 0, 64)
        sinv = self.f32(self.W_TAB + 2048, 2048, 0, 64)
        ang = cosv
        self.copy("dve", ang, posi)
        invf = self.f32(self.W_INVF, 1, 0, 64)
        sgn = self.f32(self.W_INVF + 1, 1, 0, 64)
        self.ts("dve", ang, ang, invf, ALU.mult)
        kf = sinv
        self.ts("dve", kf, ang, 1.0 / TWO_PI, ALU.mult)
        ki = self.i32(self.W_TMP, 2048, 0, 64)
        self.copy("dve", ki, kf)
        self.copy("dve", kf, ki)
        self.stt(ang, kf, -CW1, ang, ALU.mult, ALU.add)
        self.stt(ang, kf, -CW2, ang, ALU.mult, ALU.add)
        m1 = self.f32(self.W_TMP, 2048, 0, 64)
        self.range_wrap(m1, ang, 0.0, kf)
        self.act(sinv, m1, AF.Sin, scale=sgn)
        m2 = self.f32(self.W_TMP, 2048, 0, 64)
        self.range_wrap(m2, m1, math.pi / 2, ang)
        self.act(cosv, m2, AF.Sin)

    def mla_load_weights(self):
        dr = self.dr
        w = self.bf(self.W_WIN_M, 0, 8 * 640)
        self.wload(w.w(w.ap.rearrange("p (k n) -> p k n", k=8)),
                   dr["mla_w_in"].rearrange("(k p) n -> p k n", p=128))
        w = self.bf(self.W_WUQ, 0, 2 * 2048)
        self.wload(w.w(w.ap.rearrange("p (k n) -> p k n", k=2)),
                   dr["mla_w_uq"].rearrange("(k p) n -> p k n", p=128))
        w = self.bf(self.W_WUK, 0, 2 * 1024)
        self.wload(w.w(w.ap.rearrange("p (k n) -> p k n", k=2)),
                   dr["mla_w_uk"].rearrange("(k p) n -> p k n", p=128))
        w = self.bf(self.W_WUV, 0, 2 * 1024)
        self.wload(w.w(w.ap.rearrange("p (k n) -> p k n", k=2)),
                   dr["mla_w_uv"].rearrange("(k p) n -> p k n", p=128))

    def rms_latent(self, st, oc_pair, wbase, gcol_w, dst_w):
        hraw = []
        sqs = []
        for i, oc in enumerate(oc_pair):
            b = self.nbank()
            pso = self.ps(b)
            for k in range(8):
                self.mm(pso, self.bf(wbase, k * 640 + oc * 128, 128), self.UT(k, st * 512, 512), k == 0, k == 7)
            sq = self.bf(self.tmp(), 0, 512)
            self.act(sq, pso, AF.Square)
            hr = self.f32(self.tmp(), 512)
            self.copy("dve", hr, pso)
            hraw.append(hr)
            sqs.append(sq)
        b = self.nbank()
        ssq = self.ps(b)
        self.mm(ssq, self.bf(self.W_ONB, 0, 128), sqs[0], True, False)
        self.mm(ssq, self.bf(self.W_ONB, 0, 128), sqs[1], False, True)
        ms = self.f32(self.tmp(), 512)
        self.act(ms, ssq, AF.Sqrt, scale=1.0 / 256.0, bias=EPS)
        rstd = self.f32(self.tmp(), 512)
        self.P.op("dve", lambda e, o=rstd.ap, a=ms.ap: e.reciprocal(out=o, in_=a), reads=[ms], writes=[rstd])
        for i in range(2):
            dst = self.bf(dst_w, i * 2048 + st * 512, 512)
            self.stt(dst, hraw[i], self.f32(gcol_w + i, 1), rstd, ALU.mult, ALU.mult)

    def mla_latents_rms(self, st):
        self.rms_latent(st, (0, 1), self.W_WIN_M, self.W_GQ, self.W_CQ)
        self.rms_latent(st, (2, 3), self.W_WIN_M, self.W_GKV, self.W_CKV)

    def mla_latents_kr(self, st):
        ba = self.nbank()
        bb = self.nbank()
        pa = self.ps(ba, 0, 512, 0, 64)
        pb = self.ps(bb, 0, 512, 0, 64)
        for k in range(8):
            self.mm(pa, self.bf(self.W_WIN_M, k * 640 + 512, 64), self.UT(k, st * 512, 512), k == 0, k == 7)
        for k in range(8):
            self.mm(pb, self.bf(self.W_WIN_M, k * 640 + 576, 64), self.UT(k, st * 512, 512), k == 0, k == 7)
        self.rope_combine(pa, pb, st, self.bf(self.W_KR, st * 512, 512, 0, 64))

    def rope_combine(self, pa, pb, st, dst):
        t1 = self.f32(self.tmp(), 512, 0, 64)
        t2 = self.f32(self.tmp(), 512, 0, 64)
        self.tt("dve", t1, pa, self.f32(self.W_TAB + st * 512, 512, 0, 64), ALU.mult)
        self.tt("dve", t2, pb, self.f32(self.W_TAB + 2048 + st * 512, 512, 0, 64), ALU.mult)
        self.tt("dve", dst, t1, t2, ALU.add)

    def mla_head_proj(self, h):
        for st in range(4):
            t0 = st * 512
            b = self.nbank()
            pso = self.ps(b)
            for k in range(2):
                self.mm(pso, self.bf(self.W_WUQ, k * 2048 + h * 256, 128), self.bf(self.W_CQ, k * 2048 + t0, 512), k == 0, k == 1)
            self.copy("act", self.bf(self.W_QN, t0, 512), pso)
            pa = self.ps(self.nbank(), 0, 512, 0, 64)
            for k in range(2):
                self.mm(pa, self.bf(self.W_WUQ, k * 2048 + h * 256 + 128, 64), self.bf(self.W_CQ, k * 2048 + t0, 512), k == 0, k == 1)
            t1 = self.f32(self.tmp(), 512, 0, 64)
            self.tt("dve", t1, pa, self.f32(self.W_TAB + t0, 512, 0, 64), ALU.mult)
            pb = self.ps(self.nbank(), 0, 512, 0, 64)
            for k in range(2):
                self.mm(pb, self.bf(self.W_WUQ, k * 2048 + h * 256 + 192, 64), self.bf(self.W_CQ, k * 2048 + t0, 512), k == 0, k == 1)
            t2 = self.f32(self.tmp(), 512, 0, 64)
            self.tt("dve", t2, pb, self.f32(self.W_TAB + 2048 + t0, 512, 0, 64), ALU.mult)
            self.tt("dve", self.bf(self.W_QR, t0, 512, 0, 64), t1, t2, ALU.add)
            b = self.nbank()
            pso = self.ps(b)
            for k in range(2):
                self.mm(pso, self.bf(self.W_WUK, k * 1024 + h * 128, 128), self.bf(self.W_CKV, k * 2048 + t0, 512), k == 0, k == 1)
            self.copy("dve", self.bf(self.W_KN, t0, 512), pso)
        for bq in range(4):
            b = self.nbank()
            for j in range(4):
                blk = bq * 4 + j
                pso = self.ps(b, j * 128, 128)
                for k in range(2):
                    self.mm(pso, self.bf(self.W_CKV, k * 2048 + blk * 128, 128), self.bf(self.W_WUV, k * 1024 + h * 128, 128), k == 0, k == 1)
            self.copy("act", self.bf(self.W_VH, bq * 512, 512), self.ps(b))

    def attn_core(self, s_fn, pv_fn, fin_fn, sbanks):
        la = len(sbanks) - 1
        tiles = []
        for i in range(4):
            nj = 4 * i + 4
            for j in range(nj):
                c0 = (j - 4 * i) * 128 if j >= 4 * i else 0
                tiles.append((i, j, c0, nj))
        n = len(tiles)
        for t in range(n + la):
            if t < n:
                i, j, c0, nj = tiles[t]
                s_fn(i, j, c0, sbanks[t % len(sbanks)], t % 4)
            u = t - la
            if u >= 0:
                i, j, c0, nj = tiles[u]
                pv_fn(i, j, c0, u % 4, nj)
                if j == nj - 1:
                    fin_fn(i)

    def mask_mm(self, S_diag):
        self.mm(S_diag, self.bf(self.W_NIB, 0, 128), self.bf(self.W_UTB, 0, 128), False, True)

    def mla_head_attn(self, h, ao_chunk):
        def s_fn(i, j, c0, sb, pt):
            n = 512 - c0
            S_ = self.ps(sb, c0, n)
            diag = j >= 4 * i
            self.mm(S_, self.bf(self.W_KN, j * 128, 128), self.bf(self.W_QN, i * 512 + c0, n), True, False)
            self.mm(S_, self.bf(self.W_KR, j * 128, 128, 0, 64), self.bf(self.W_QR, i * 512 + c0, n, 0, 64), False, not diag)
            if diag:
                self.mask_mm(self.ps(sb, c0, 128))
            P_ = self.PT(pt, c0, n)
            self.act(P_, S_, AF.Exp, scale=MLA_SCALE)

        def pv_fn(i, j, c0, pt, nj):
            n = 512 - c0
            P_ = self.PT(pt, c0, n)
            self.mm(self.ps(2 + i % 2, c0, n), self.bf(self.W_VH, j * 128, 128), P_, j == 0, j == nj - 1)
            self.mm(self.ps(4 + i % 2, c0, n), self.bf(self.W_ONB, 0, 128), P_, j == 0, j == nj - 1)

        def fin_fn(i):
            r = self.f32(self.tmp(), 512)
            self.P.op("dve", lambda e, o=r.ap, a=self.ps(4 + i % 2).ap: e.reciprocal(out=o, in_=a),
                      reads=[self.ps(4 + i % 2)], writes=[r])
            self.tt("dve", self.UT(ao_chunk, i * 512, 512), self.ps(2 + i % 2), r, ALU.mult)

        self.attn_core(s_fn, pv_fn, fin_fn, (0, 1, 6))

    def mla_mixer(self, seq, tail_ln=None):
        dr = self.dr
        self.rope_tables(seq)
        self.cut("p2")
        for st in range(4):
            self.mla_latents_kr(st)
        self.cut("p3")
        self.load_lnv(0, 0, seq, gate_only=True)
        for g in range(2):
            wo = self.bf(self.W_WO_M, 0, 4 * 1024)
            self.wload(wo.w(wo.ap.rearrange("p (k n) -> p k n", k=4)),
                       dr["mla_w_o"][g * 512:(g + 1) * 512, :].rearrange("(k p) n -> p k n", p=128))
            for hh in range(4):
                h = 4 * g + hh
                self.mla_head_proj(h)
                self.cut("p4")
                self.mla_head_attn(h, 4 * g + hh)
                self.cut("p5")
            tail = None
            if g == 1 and tail_ln is not None:
                tail = tail_ln()
            for blk in range(NB):
                for half in range(2):
                    b = self.nbank()
                    pso = self.ps(b)
                    for hh in range(4):
                        self.mm(pso, self.UT(4 * g + hh, blk * 128, 128), self.bf(self.W_WO_M, hh * 1024 + half * 512, 512), hh == 0, hh == 3)
                    self.resid(g == 0, blk, half, pso)
                if tail is not None:
                    tail[0](blk)
            if tail is not None:
                tail[1]()

    def ffn_load_gu(self, layer, cg):
        dr = self.dr
        slot = cg % 2
        wg = self.bf(self.W_WG + 1024 * slot, 0, 2048)
        wu = self.bf(self.W_WU + 1024 * slot, 0, 2048)
        self.wload(wg.w(wg.ap.rearrange("p (k n) -> p k n", k=8)),
                   dr["ffn_w_gate"][layer, :, cg * 256:(cg + 1) * 256].rearrange("(k p) n -> p k n", p=128))
        self.wload(wu.w(wu.ap.rearrange("p (k n) -> p k n", k=8)),
                   dr["ffn_w_up"][layer, :, cg * 256:(cg + 1) * 256].rearrange("(k p) n -> p k n", p=128))

    def ffn_gu(self, cg, jj, st, j0):
        slot = cg % 2
        j = 2 * cg + jj
        bg = self.nbank()
        bu = self.nbank()
        pg = self.ps(bg)
        pu = self.ps(bu)
        for k in range(8):
            self.mm(pg, self.bf(self.W_WG + 1024 * slot, k * 256 + jj * 128, 128), self.UT(k, st * 512, 512), k == 0, k == 7)
        for k in range(8):
            self.mm(pu, self.bf(self.W_WU + 1024 * slot, k * 256 + jj * 128, 128), self.UT(k, st * 512, 512), k == 0, k == 7)
        t = self.f32(self.tmp(), 512)
        self.act(t, pg, AF.Silu)
        self.tt("dve", self.bf(self.W_HT + 1024 * self.ht_slot(j, j0), st * 512, 512), t, pu, ALU.mult)

    @staticmethod
    def ht_slot(j, j0):
        return (j - j0 + 4) % 8 if j0 == 0 else j - j0

    def ffn_hook(self, layer):
        def hook(st):
            return [lambda cg=cg, jj=jj: self.ffn_gu(cg, jj, st, 0) for cg in (0, 1) for jj in range(2)]
        return hook

    def ffn(self, layer, seq, pre=0, tail_ln=None):
        dr = self.dr
        wd_d = dr["ffn_w_down"]
        groups = [(0, 8), (8, 16), (16, 22)]
        dcnt = 0
        for gi, (j0, j1) in enumerate(groups):
            for cg in range(j0 // 2, j1 // 2):
                if cg < pre:
                    continue
                self.ffn_load_gu(layer, cg)
                for jj in range(2):
                    for st in range(4):
                        self.ffn_gu(cg, jj, st, j0)
            nch = j1 - j0
            last = gi == len(groups) - 1

            def load_wd(half, slot):
                wd = self.bf(self.W_WD + 2048 * slot, 0, nch * 512)
                self.wload(wd.w(wd.ap.rearrange("p (k n) -> p k n", k=nch)),
                           wd_d[layer, j0 * 128:j1 * 128, half * 512:(half + 1) * 512].rearrange("(k p) n -> p k n", p=128))

            def down(blk, half, slot):
                b = self.nbank()
                pso = self.ps(b)
                for jl in range(nch):
                    self.mm(pso, self.bf(self.W_HT + 1024 * self.ht_slot(j0 + jl, j0), blk * 128, 128),
                            self.bf(self.W_WD + 2048 * slot, jl * 512, 512), jl == 0, jl == nch - 1)
                self.resid(gi == 0, blk, half, pso)

            if last and tail_ln is not None:
                load_wd(0, dcnt % 2)
                load_wd(1, (dcnt + 1) % 2)
                step, flush = tail_ln
                for blk in range(NB):
                    down(blk, 0, dcnt % 2)
                    down(blk, 1, (dcnt + 1) % 2)
                    step(blk)
                flush()
                dcnt += 2
            else:
                for half in range(2):
                    slot = dcnt % 2
                    dcnt += 1
                    load_wd(half, slot)
                    for blk in range(NB):
                        down(blk, half, slot)

    def fox_pro_subtile(self, st):
        t0 = st * 512
        b = self.nbank()
        pso = self.ps(b, 0, 512, 0, 16)
        for k in range(8):
            self.mm(pso, self.bf(self.W_WF, k * 16, 16), self.UT(k, t0, 512), k == 0, k == 7)
        e_ = self.f32(self.tmp(), 512, 0, 16)
        self.act(e_, pso, AF.Exp, scale=-1.0, bias=self.f32(self.W_BF + 1, 1, 0, 16))
        self.act(e_, e_, AF.Ln, scale=1.0, bias=1.0)
        g2w = self.W_GC + 512 * (st % 2)
        g2 = self.f32(g2w, 512, 0, 16)
        if st == 0:
            self.P.op("dve", lambda e, o=g2.ap, a=e_.ap: e.tensor_tensor_scan(o, a, a, 0.0, ALU.add, ALU.max),
                      reads=[e_], writes=[g2])
        else:
            ini = self.f32(self.W_GC + 512 * ((st - 1) % 2) + 511, 1, 0, 16)
            self.P.op("dve", lambda e, o=g2.ap, a=e_.ap, i=ini.ap: e.tensor_tensor_scan(o, a, a, i, ALU.add, ALU.max),
                      reads=[e_, ini], writes=[g2])
        b = self.nbank()
        for jb in range(4):
            self.tr(self.ps(b, jb * 16, 16), self.f32(g2w + jb * 128, 128, 0, 16), self.IDF(16))
        self.copy("dve", self.f32(self.W_GTOK + st * 64, 64), self.ps(b, 0, 64))
        ft = self.W_FT
        hi = self.bf(self.W_FSP, t0, 512, 0, 16)
        self.ts("dve", hi, g2, -8.0, ALU.mult)
        r1 = e_
        self.stt(r1, g2, -8.0, hi, ALU.mult, ALU.subtract)
        mid = self.bf(ft, 0, 512, 0, 16)
        self.copy("dve", mid, r1)
        self.tt("dve", r1, r1, mid, ALU.subtract)
        lo = self.bf(ft + 256, 0, 512, 0, 16)
        self.copy("dve", lo, r1)
        self.copy("pool", self.bf(self.W_FSP, t0, 512, 32, 48), mid)
        self.copy("pool", self.bf(self.W_FSP, t0, 512, 64, 80), lo)

    def fox_setup(self):
        self.memset("pool", self.bf(self.W_KA, 0, 2048, 64, 67), 1.0)
        self.memset("pool", self.bf(self.W_KA + 1024, 0, 2048, 64, 67), 1.0)
        vg = self.bf(self.W_VG, 0, 8192)
        v5 = vg.ap.rearrange("p (b q t e) -> p b q t e", b=16, q=2, t=2)
        self.memset("pool", vg.w(v5[:, :, :, 0, 64:128]), 1.0)
        self.memset("pool", vg.w(v5[:, :, :, 1, 0:64]), 1.0)

    def fox_load_group(self, g):
        dr = self.dr
        win = dr["fox_w_in"]
        for part in range(3):
            w = self.bf(self.W_WIN_F, 0, 8 * 768)
            w3 = w.ap.rearrange("p (k n) -> p k n", k=8)
            c0 = part * 1024 + g * 256
            self.wload(w.w(w3[:, :, part * 256:(part + 1) * 256]),
                       win[:, c0:c0 + 256].rearrange("(k p) n -> p k n", p=128))
        wo = self.bf(self.W_WO_F, 0, 2048)
        self.wload(wo.w(wo.ap.rearrange("p (k n) -> p k n", k=2)),
                   dr["fox_w_o"][g * 256:(g + 1) * 256, :].rearrange("(k p) n -> p k n", p=128))

    def fox_vproj_unit(self, b2):
        vg = self.bf(self.W_VG, 0, 8192)
        v5 = vg.ap.rearrange("p (b q t e) -> p b q t e", b=16, q=2, t=2)
        b = self.nbank()
        for jb in range(2):
            blk = 2 * b2 + jb
            pso = self.ps(b, jb * 256, 256)
            for k in range(8):
                self.mm(pso, self.UT(k, blk * 128, 128), self.bf(self.W_WIN_F, k * 768 + 512, 256), k == 0, k == 7)
        psv = self.ps(b)
        p5 = psv.ap.rearrange("p (b q t d) -> p b q t d", b=2, q=2, t=2)
        self.copy("dve", vg.w(v5[:, 2 * b2:2 * b2 + 2, :, 0, 0:64]), psv.w(p5[:, :, :, 0, :]))
        self.copy("dve", vg.w(v5[:, 2 * b2:2 * b2 + 2, :, 1, 64:128]), psv.w(p5[:, :, :, 1, :]))

    def fox_qk_proj(self, pair, st):
        t0 = st * 512
        for which, base_w, col0 in ((0, self.W_QA, 0), (1, self.W_KA, 256)):
            b = self.nbank()
            pp = self.ps(b)
            for k in range(8):
                self.mm(pp, self.bf(self.W_WIN_F, k * 768 + col0 + pair * 128, 128), self.UT(k, t0, 512), k == 0, k == 7)
            self.copy("dve", self.bf(base_w, t0, 512, 0, 64), self.ps(b, 0, 512, 0, 64))
            self.copy("dve", self.bf(base_w + 1024, t0, 512, 0, 64), self.ps(b, 0, 512, 64, 128))

    def fox_hook(self):
        def hook(st):
            return [lambda: self.fox_pro_subtile(st),
                    lambda: self.fox_vproj_unit(2 * st),
                    lambda: self.fox_vproj_unit(2 * st + 1),
                    lambda: self.fox_qk_proj(0, st)]
        return hook

    def fox_group(self, seq, g, pre=False, tail_ln=None):
        if not pre:
            self.fox_load_group(g)
            for b2 in range(8):
                self.fox_vproj_unit(b2)
        for pair in range(2):
            if not (pre and pair == 0):
                for st in range(4):
                    self.fox_qk_proj(pair, st)
            for ab in range(2):
                hh = 2 * pair + ab
                h = 4 * g + hh
                a = self.W_FSP
                src = self.arena[h:h + 65:32, a:a + 1024].bitcast(BF16)
                qa_aug = self.bf(self.W_QA + 1024 * ab, 0, 2048, 64, 67)
                self.dma("sp", qa_aug.ap, src, reads=[self.bf(self.W_FSP, 0, 2048, 0, 80)], writes=[qa_aug])
            for ab in range(2):
                hh = 2 * pair + ab
                self.fox_head_attn(4 * g + hh, hh, g, ab)
        aob = self.W_AO
        tail = None
        if g == 3 and tail_ln is not None:
            tail = tail_ln()
        for blk in range(NB):
            for half in range(2):
                b = self.nbank()
                pso = self.ps(b)
                for c in range(2):
                    self.mm(pso, self.bf(aob, c * 2048 + blk * 128, 128), self.bf(self.W_WO_F, c * 1024 + half * 512, 512), c == 0, c == 1)
                self.resid(g == 0, blk, half, pso)
            if tail is not None:
                tail[0](blk)
        if tail is not None:
            tail[1]()

    def fox_head_attn(self, h, hh, g, ab):
        vg = self.bf(self.W_VG, 0, 8192)
        v5 = vg.ap.rearrange("p (b q e) -> p b q e", b=16, q=4)
        aob = self.W_AO
        odd = hh % 2
        qaw = self.W_QA + 1024 * ab
        kaw = self.W_KA + 1024 * ab

        def s_fn(i, j, c0, sb, pt):
            n = 512 - c0
            S_ = self.ps(sb, c0, n)
            diag = j >= 4 * i
            self.mm(S_, self.bf(kaw, j * 128, 128, 0, 67), self.bf(qaw, i * 512 + c0, n, 0, 67), True, not diag)
            if diag:
                self.mask_mm(self.ps(sb, c0, 128))
            P_ = self.PT(pt, c0, n)
            self.act(P_, S_, AF.Exp, scale=0.125, bias=self.f32(self.W_GTOK + j * 16 + h, 1))

        def pv_fn(i, j, c0, pt, nj):
            n = 512 - c0
            P_ = self.PT(pt, c0, n)
            lhs = vg.w(v5[:, j, hh, :])
            self.mm(self.ps(2 + i % 2, c0, n), lhs, P_, j == 0, j == nj - 1)

        def fin_fn(i):
            ob = 2 + i % 2
            tw = self.tmp()
            if not odd:
                r = self.f32(tw, 512, 0, 64)
                lsrc = self.ps(ob, 0, 512, 64, 128)
                osrc = self.ps(ob, 0, 512, 0, 64)
                dst = self.bf(aob, (hh // 2) * 2048 + i * 512, 512, 0, 64)
            else:
                r = self.f32(tw, 512, 64, 128)
                lsrc = self.ps(ob, 0, 512, 0, 64)
                osrc = self.ps(ob, 0, 512, 64, 128)
                dst = self.bf(aob, (hh // 2) * 2048 + i * 512, 512, 64, 128)
            self.P.op("dve", lambda e, o=r.ap, a=lsrc.ap: e.reciprocal(out=o, in_=a), reads=[lsrc], writes=[r])
            self.tt("dve", dst, osrc, r, ALU.mult)

        self.attn_core(s_fn, pv_fn, fin_fn, (0, 1, 4))

    def fox_mixer(self, seq, tail_ln=None):
        self.load_lnv(1, 0, seq, gate_only=True)
        for g in range(4):
            self.fox_group(seq, g, pre=(g == 0), tail_ln=tail_ln)

    def cut(self, name):
        if self.stop_after == name:
            raise StopBuild()

    def forward(self):
        try:
            self._forward()
        except StopBuild:
            pass
        self.P.finish()

    def _forward(self):
        self.prologue()
        self.cut("p0")
        front_done = False
        for seq in range(self.nseq):
            if not front_done:
                self.seq_front(seq)
            front_done = False
            self.cut("p1")
            if self.stop_after == "l0a":
                self.mla_mixer(seq)
                self.load_lnv(0, 0, seq, ln_only=True)
                self.ln_stage(0, 0, seq, final=True)
                continue

            def tail_l0a(seq=seq):
                self.load_lnv(0, 0, seq, ln_only=True)
                self.ffn_load_gu(0, 0)
                self.ffn_load_gu(0, 1)
                return self.ln_pipe(0, 0, seq, next_mod=(0, 3, 4), hook=self.ffn_hook(0))
            self.mla_mixer(seq, tail_ln=tail_l0a)
            self.load_lnv(0, 1, seq, gate_only=True)
            self.ffn(0, seq, pre=2)
            self.load_lnv(0, 1, seq, ln_only=True)
            if self.stop_after == "l0f":
                self.ln_stage(0, 1, seq, final=True)
                continue
            self.fox_setup()
            self.fox_load_group(0)
            self.ln_stage(0, 1, seq, next_mod=(1, 0, 1), hook=self.fox_hook())
            if self.stop_after == "l1a":
                self.fox_mixer(seq)
                self.load_lnv(1, 0, seq, ln_only=True)
                self.ln_stage(1, 0, seq, final=True)
                continue

            def tail_l1a(seq=seq):
                self.load_lnv(1, 0, seq, ln_only=True)
                self.ffn_load_gu(1, 0)
                self.ffn_load_gu(1, 1)
                return self.ln_pipe(1, 0, seq, next_mod=(1, 3, 4), hook=self.ffn_hook(1))
            self.fox_mixer(seq, tail_ln=tail_l1a)
            self.load_lnv(1, 1, seq, gate_only=True)
            if seq + 1 < self.nseq:
                self.ffn(1, seq, pre=2)
                self.load_lnv(1, 1, seq, ln_only=True)
                self.mla_load_weights()
                self.ln_stage(1, 1, seq, final=True, next_seq=seq + 1)
                front_done = True
            else:
                self.load_lnv(1, 1, seq, ln_only=True)
                self.ffn(1, seq, pre=2, tail_ln=self.ln_pipe(1, 1, seq, final=True))


def build_program(stop_after=None, nseq=NSEQ):
    nc = bass.Bass("TRN2", target_bir_lowering=False)
    dram = {}

    def din(name, shape, dt=F32):
        dram[name] = nc.dram_tensor(name, list(shape), dt, kind="ExternalInput").ap()

    din("x", [NSEQ * S, D])
    din("cT", [128, 16])
    din("pos", [NSEQ, S], I32)
    din("cbf", [128, 512])
    din("gq", [128, 2])
    din("gkv", [128, 2])
    din("invf", [64, 2])
    din("fox_bf", [16, 1])
    din("mla_w_in", [D, 640])
    din("mla_w_uq", [256, 2048])
    din("mla_w_uk", [256, 1024])
    din("mla_w_uv", [256, 1024])
    din("mla_w_o", [D, D])
    din("fox_w_in", [D, 3088])
    din("fox_w_o", [D, D])
    din("ada_w", [2, D, 6 * D])
    din("ada_b", [2, 6 * D])
    din("ffn_w_gate", [2, D, DFF])
    din("ffn_w_up", [2, D, DFF])
    din("ffn_w_down", [2, DFF, D])
    din("ln_g", [2, 2, D])
    din("ln_b", [2, 2, D])
    dram["out"] = nc.dram_tensor("out", [NSEQ * S, D], F32, kind="ExternalOutput").ap()
    dram["modrows"] = nc.dram_tensor("modrows", [4, 6 * D], F32).ap()

    with ExitStack() as es:
        arena = es.enter_context(nc.sbuf_tensor("arena", [128, SB_WORDS], F32))
        psum = es.enter_context(nc.psum_tensor("psum", [128, 4096], F32))
        bld = Builder(nc, arena, psum, dram, stop_after=stop_after, nseq=nseq)
        bld.forward()
        P = bld.P
        sems = {}
        for n, E in P.eng.items():
            sems[E.key] = es.enter_context(nc.semaphore("s_" + n))
        for q in ("sp", "pool", "act"):
            for slot in range(NDMA_SEMS):
                if P.eng[q].dma_vals[slot] > 0:
                    key = "D:%s:%d" % (q, slot)
                    sems[key] = es.enter_context(nc.semaphore("d_%s_%d" % (q, slot)))
        block = es.enter_context(nc.Block())

        def emit(e, E):
            own = sems[E.key]
            for it in E.items:
                for key, val in it.waits:
                    e.wait_ge(sems[key], val)
                if it.fn is None:
                    continue
                ins = it.fn(e)
                if it.dma is not None:
                    ins.then_inc(sems[it.dma[0]], 16)
                elif it.signal:
                    ins.then_inc(own, 1)

        @block.tensor
        def _(e):
            emit(e, P.eng["pe"])

        @block.scalar
        def _(e):
            emit(e, P.eng["act"])

        @block.vector
        def _(e):
            emit(e, P.eng["dve"])

        @block.gpsimd
        def _(e):
            emit(e, P.eng["pool"])

        @block.sync
        def _(e):
            emit(e, P.eng["sp"])
    return nc, P


def host_inputs(inputs):
    f = np.float32
    x = np.ascontiguousarray(inputs["x"], dtype=f)
    c = np.asarray(inputs["c"], dtype=f)
    pos = np.ascontiguousarray(inputs["positions"]).astype(np.int32)
    w_in = np.asarray(inputs["mla_w_in"][0], dtype=f)
    kr = w_in[:, 512:576]
    kr_sw = np.concatenate([kr[:, 32:], kr[:, :32]], axis=1)
    w_in_ext = np.ascontiguousarray(np.concatenate([w_in[:, :576], kr_sw], axis=1))
    w_uq = np.asarray(inputs["mla_w_uq"][0], dtype=f).reshape(256, 8, 192)
    rope = w_uq[:, :, 128:192]
    rope_sw = np.concatenate([rope[:, :, 32:], rope[:, :, :32]], axis=2)
    w_uq_ext = np.ascontiguousarray(np.concatenate([w_uq, rope_sw], axis=2).reshape(256, 2048))
    ident = np.eye(128, dtype=f)
    ones = np.ones((128, 128), dtype=f)
    negi = (-30000.0 * ident).astype(f)
    utri = (np.arange(128)[:, None] > np.arange(128)[None, :]).astype(f)
    cbf = np.ascontiguousarray(np.concatenate([ident, ones, negi, utri], axis=1))
    half = 32
    inv = (10000.0 ** (-np.arange(half, dtype=np.float32) / half)).astype(f)
    invf = np.zeros((64, 2), dtype=f)
    invf[:, 0] = np.concatenate([inv, inv])
    invf[:32, 1] = -1.0
    invf[32:, 1] = 1.0
    shared = {
        "cbf": cbf,
        "gq": np.ascontiguousarray(np.asarray(inputs["mla_g_q"][0], dtype=f).reshape(2, 128).T),
        "gkv": np.ascontiguousarray(np.asarray(inputs["mla_g_kv"][0], dtype=f).reshape(2, 128).T),
        "invf": invf,
        "fox_bf": np.ascontiguousarray(np.asarray(inputs["fox_b_f"][0], dtype=f).reshape(16, 1)),
        "mla_w_in": w_in_ext,
        "mla_w_uq": w_uq_ext,
        "mla_w_uk": np.ascontiguousarray(inputs["mla_w_uk"][0], dtype=f),
        "mla_w_uv": np.ascontiguousarray(inputs["mla_w_uv"][0], dtype=f),
        "mla_w_o": np.ascontiguousarray(inputs["mla_w_o"][0], dtype=f),
        "fox_w_in": np.ascontiguousarray(inputs["fox_w_in"][0], dtype=f),
        "fox_w_o": np.ascontiguousarray(inputs["fox_w_o"][0], dtype=f),
        "ada_w": np.ascontiguousarray(inputs["ada_w"], dtype=f),
        "ada_b": np.ascontiguousarray(inputs["ada_b"], dtype=f),
        "ffn_w_gate": np.ascontiguousarray(inputs["ffn_w_gate"], dtype=f),
        "ffn_w_up": np.ascontiguousarray(inputs["ffn_w_up"], dtype=f),
        "ffn_w_down": np.ascontiguousarray(inputs["ffn_w_down"], dtype=f),
        "ln_g": np.ascontiguousarray(inputs["ln_g"], dtype=f),
        "ln_b": np.ascontiguousarray(inputs["ln_b"], dtype=f),
    }
    in_maps = []
    for core in range(NCORES):
        b0 = core * NSEQ
        m = dict(shared)
        m["x"] = np.ascontiguousarray(x[b0:b0 + NSEQ].reshape(NSEQ * S, D))
        cc = c[b0:b0 + NSEQ]
        m["cT"] = np.ascontiguousarray(cc.reshape(NSEQ, 8, 128).transpose(2, 1, 0).reshape(128, 16))
        m["pos"] = np.ascontiguousarray(pos[b0:b0 + NSEQ])
        in_maps.append(m)
    return in_maps


def kernel(**inputs):
    in_maps = host_inputs(inputs)
    nc, _ = build_program()
    res = run_bass_kernel_spmd(nc, in_maps, core_ids=list(range(NCORES)))
    outs = [np.asarray(r["out"]).reshape(NSEQ, S, D) for r in res.results]
    return np.concatenate(outs, axis=0).astype(np.float32)
```

```python
import math
from bisect import bisect_left
from contextlib import ExitStack

import numpy as np
import concourse.bass as bass
import concourse.mybir as mybir
from concourse.bass_utils import run_bass_kernel_spmd

F32 = mybir.dt.float32
BF16 = mybir.dt.bfloat16
I32 = mybir.dt.int32
ALU = mybir.AluOpType
AF = mybir.ActivationFunctionType

D = 1024
S = 2048
NSEQ = 2
NB = 16
DFF = 2816
NCORES = 8
ALPHA = (2.0 * 2) ** 0.25
EPS = 1e-5
MLA_SCALE = 192.0 ** -0.5
TWO_PI = 2.0 * math.pi
CW1 = 6.28125
CW2 = TWO_PI - 6.28125

GRAN = 256
SB_WORDS = 52500
NDMA_SEMS = 12


class Item:
    __slots__ = ("fn", "waits", "signal", "dma")

    def __init__(self, fn, waits, dma=None):
        self.fn = fn
        self.waits = waits
        self.signal = False
        self.dma = dma


class Eng:
    def __init__(self, name):
        self.name = name
        self.key = "E:" + name
        self.items = []
        self.waited = {}
        self.dma_n = 0
        self.dma_vals = [0] * NDMA_SEMS
        self.needed = set()
        self.ticket_of = None


class DmaSrc:
    __slots__ = ("key", "val")

    def __init__(self, key, val):
        self.key = key
        self.val = val


class V:
    __slots__ = ("ap", "sp", "g0", "g1")

    def __init__(self, ap, sp, g0, g1):
        self.ap = ap
        self.sp = sp
        self.g0 = g0
        self.g1 = g1

    def w(self, ap):
        return V(ap, self.sp, self.g0, self.g1)


class Prog:
    def __init__(self):
        self.eng = {n: Eng(n) for n in ("pe", "act", "dve", "pool", "sp")}
        self.spaces = {}

    def space(self, name, ngran):
        self.spaces[name] = ([None] * ngran, [None] * ngran)

    @staticmethod
    def _upd(need, src, seq):
        o = need.get(src)
        if o is None or o < seq:
            need[src] = seq

    def _deps(self, E, reads, writes):
        need = {}
        upd = self._upd
        for r in reads:
            lastw, _ = self.spaces[r.sp]
            for g in range(r.g0, r.g1):
                w = lastw[g]
                if w is not None:
                    upd(need, w[0], w[1])
        for r in writes:
            lastw, rd = self.spaces[r.sp]
            for g in range(r.g0, r.g1):
                w = lastw[g]
                if w is not None:
                    upd(need, w[0], w[1])
                rr = rd[g]
                if rr:
                    for s_, q in rr.items():
                        upd(need, s_, q)
        return need

    def _waits(self, E, need, self_ok):
        waits = []
        for src, seq in need.items():
            if src is E and self_ok:
                continue
            if isinstance(src, DmaSrc):
                if E.waited.get(src.key, 0) >= src.val:
                    continue
                E.waited[src.key] = src.val
                waits.append((src.key, src.val))
            else:
                if E.waited.get(src.key, -1) >= seq:
                    continue
                E.waited[src.key] = seq
                src.needed.add(seq)
                waits.append((src, seq))
        return waits

    def _mark(self, src, seq, reads, writes):
        for r in reads:
            _, rd = self.spaces[r.sp]
            for g in range(r.g0, r.g1):
                d = rd[g]
                if d is None:
                    rd[g] = {src: seq}
                else:
                    d[src] = seq
        for r in writes:
            lastw, rd = self.spaces[r.sp]
            t = (src, seq)
            for g in range(r.g0, r.g1):
                lastw[g] = t
                rd[g] = None

    def op(self, eng, fn, reads=(), writes=()):
        E = self.eng[eng]
        ps_reads = [r for r in reads if r.sp == "ps"]
        if ps_reads:
            writes = list(writes) + ps_reads
        need = self._deps(E, reads, writes)
        waits = self._waits(E, need, self_ok=(eng == "pe"))
        seq = len(E.items)
        E.items.append(Item(fn, waits))
        if ps_reads:
            reads = [r for r in reads if r.sp != "ps"]
        self._mark(E, seq, reads, writes)

    def dma(self, q, fn, reads=(), writes=()):
        E = self.eng[q]
        slot = E.dma_n % NDMA_SEMS
        E.dma_n += 1
        key = "D:%s:%d" % (q, slot)
        prev = E.dma_vals[slot]
        val = prev + 16
        E.dma_vals[slot] = val
        need = self._deps(None, reads, writes)
        waits = self._waits(E, need, self_ok=False)
        if prev > 0 and E.waited.get(key, 0) < prev:
            E.waited[key] = prev
            waits.append((key, prev))
        E.items.append(Item(fn, waits, dma=(key, val)))
        self._mark(DmaSrc(key, val), 0, reads, writes)

    def finish(self):
        sp = self.eng["sp"]
        waits = []
        for q in ("sp", "pool", "act"):
            E = self.eng[q]
            for slot in range(NDMA_SEMS):
                v = E.dma_vals[slot]
                key = "D:%s:%d" % (q, slot)
                if v > 0 and sp.waited.get(key, 0) < v:
                    sp.waited[key] = v
                    waits.append((key, v))
        sp.items.append(Item(None, waits))
        self.resolve()

    def resolve(self):
        for E in self.eng.values():
            order = sorted(E.needed)
            E.ticket_of = {seq: i + 1 for i, seq in enumerate(order)}
            for seq in order:
                assert E.items[seq].dma is None
                E.items[seq].signal = True
            E.sig = order
        for E in self.eng.values():
            for it in E.items:
                it.waits = [(w[0].key, w[0].ticket_of[w[1]]) if isinstance(w[0], Eng) else w for w in it.waits]


class StopBuild(Exception):
    pass


class Builder:
    def __init__(self, nc, arena, psum, dram, stop_after=None, nseq=NSEQ):
        self.nc = nc
        self.arena = arena
        self.psum = psum
        self.dr = dram
        self.P = Prog()
        self.P.space("sb", (SB_WORDS * 4 + GRAN - 1) // GRAN)
        self.P.space("ps", 8)
        self.P.space("modrows", 4)
        self.P.space("out", NSEQ * NB)
        self.stop_after = stop_after
        self.nseq = nseq
        self.rr = 0
        self.tmp_rr = 0
        self.layout()

    def f32(self, w0, n, p0=0, p1=128):
        ap = self.arena[p0:p1, w0:w0 + n]
        return V(ap, "sb", (w0 * 4) // GRAN, (4 * (w0 + n) + GRAN - 1) // GRAN)

    def i32(self, w0, n, p0=0, p1=128):
        v = self.f32(w0, n, p0, p1)
        return v.w(v.ap.bitcast(I32))

    def bf(self, w0, e0, n, p0=0, p1=128):
        assert e0 % 2 == 0 and n % 2 == 0
        a = w0 + e0 // 2
        b = a + n // 2
        ap = self.arena[p0:p1, a:b].bitcast(BF16)
        return V(ap, "sb", (a * 4) // GRAN, (4 * b + GRAN - 1) // GRAN)

    def ps(self, bank, c0=0, n=512, p0=0, p1=128):
        a = bank * 512 + c0
        assert c0 + n <= 512
        ap = self.psum[p0:p1, a:a + n]
        return V(ap, "ps", bank, bank + 1)

    def psb(self, bank, e0, n, p0=0, p1=128):
        a = bank * 512 + e0 // 2
        b = a + n // 2
        assert b <= (bank + 1) * 512
        ap = self.psum[p0:p1, a:b].bitcast(BF16)
        return V(ap, "ps", bank, bank + 1)

    def nbank(self):
        b = self.rr
        self.rr = (self.rr + 1) % 8
        return b

    def tmp(self):
        i = self.tmp_rr
        self.tmp_rr = (self.tmp_rr + 1) % 6
        return self.W_TMP + 512 * i

    def layout(self):
        self.W_X = 0
        self.W_UT = 16384
        c = 24576
        self.W_IDB = c
        self.W_ONB = c + 64
        self.W_NIB = c + 128
        self.W_UTB = c + 1600
        self.W_STAT2 = c + 1664
        self.W_IDF = c + 192
        self.W_NEGH = c + 320
        self.W_MODT = c + 832
        self.W_GQ = c + 1024
        self.W_GKV = c + 1088
        self.W_INVF = c + 1152
        self.W_BF = c + 1216
        self.W_STAT = c + 1280
        self.W_CACT = c + 1408
        self.W_CT = c + 1472
        self.W_WF = c + 1536
        self.W_TMP = 26624
        self.W_LNV = 29696
        self.W_XB = 32768
        self.W_TAB = self.W_LNV + 1024
        self.W_PT = 34816
        ph = 35840
        self.W_PH = ph
        self.W_CQ = ph
        self.W_CKV = ph + 2048
        self.W_KR = ph + 4096
        self.W_QN = ph + 5120
        self.W_QR = ph + 6144
        self.W_KN = ph + 7168
        self.W_VH = ph + 8192
        self.W_WIN_M = ph + 5120
        self.W_WUQ = ph + 9216
        self.W_WUK = ph + 11264
        self.W_WUV = ph + 12288
        self.W_WO_M = ph + 13312
        self.W_AO = ph
        self.W_QA = ph + 2048
        self.W_KA = ph + 4096
        self.W_VG = ph + 6144
        self.W_WIN_F = ph + 10240
        self.W_WO_F = ph + 13312
        self.W_GC = ph + 14336
        self.W_FSP = ph + 15360
        self.W_GTOK = ph + 16384
        self.W_FT = self.W_PT
        self.W_HT = ph
        self.W_WG = ph + 8192
        self.W_WU = ph + 10240
        self.W_WD = ph + 12288
        self.W_ADAW = ph
        self.W_MODROW = ph + 4096
        self.W_ADAB = ph + 10240
        assert ph + 16640 <= SB_WORDS

    def mm(self, out, lhsT, rhs, start, stop):
        self.P.op("pe", lambda e, o=out.ap, l=lhsT.ap, r=rhs.ap, a=start, b=stop:
                  e.matmul(o, l, r, start=a, stop=b), reads=[lhsT, rhs], writes=[out])

    def tr(self, out, in_, ident):
        self.P.op("pe", lambda e, o=out.ap, i=in_.ap, d=ident.ap: e.transpose(o, i, d),
                  reads=[in_, ident], writes=[out])

    def act(self, out, in_, func, scale=1.0, bias=0.0):
        reads = [in_]
        sc = scale
        bi = bias
        if isinstance(scale, V):
            reads.append(scale)
            sc = scale.ap
        if isinstance(bias, V):
            reads.append(bias)
            bi = bias.ap
        self.P.op("act", lambda e, o=out.ap, i=in_.ap, f=func, s=sc, b=bi:
                  e.activation(out=o, in_=i, func=f, scale=s, bias=b), reads=reads, writes=[out])

    def tt(self, eng, out, in0, in1, op):
        self.P.op(eng, lambda e, o=out.ap, a=in0.ap, b=in1.ap, p=op:
                  e.tensor_tensor(out=o, in0=a, in1=b, op=p), reads=[in0, in1], writes=[out])

    def ts(self, eng, out, in0, s1, op0, s2=None, op1=None):
        reads = [in0]
        a1 = s1
        a2 = s2
        if isinstance(s1, V):
            reads.append(s1)
            a1 = s1.ap
        if isinstance(s2, V):
            reads.append(s2)
            a2 = s2.ap
        if op1 is None:
            self.P.op(eng, lambda e, o=out.ap, a=in0.ap, x=a1, p0=op0:
                      e.tensor_scalar(out=o, in0=a, scalar1=x, scalar2=None, op0=p0),
                      reads=reads, writes=[out])
        else:
            self.P.op(eng, lambda e, o=out.ap, a=in0.ap, x=a1, y=a2, p0=op0, p1=op1:
                      e.tensor_scalar(out=o, in0=a, scalar1=x, scalar2=y, op0=p0, op1=p1),
                      reads=reads, writes=[out])

    def stt(self, out, in0, scalar, in1, op0, op1):
        reads = [in0, in1]
        sc = scalar
        if isinstance(scalar, V):
            reads.append(scalar)
            sc = scalar.ap
        self.P.op("dve", lambda e, o=out.ap, a=in0.ap, s=sc, b=in1.ap, p0=op0, p1=op1:
                  e.scalar_tensor_tensor(out=o, in0=a, scalar=s, in1=b, op0=p0, op1=p1),
                  reads=reads, writes=[out])

    def copy(self, eng, out, in_):
        if eng == "act":
            self.act(out, in_, AF.Copy)
        else:
            self.P.op(eng, lambda e, o=out.ap, i=in_.ap: e.tensor_copy(out=o, in_=i),
                      reads=[in_], writes=[out])

    def memset(self, eng, out, val):
        self.P.op(eng, lambda e, o=out.ap, v=val: e.memset(o, v), writes=[out])

    def dma(self, q, out_ap, in_ap, reads, writes, nc_ok=False):
        if nc_ok:
            self.P.dma(q, lambda e, o=out_ap, i=in_ap: e.dma_start(out=o, in_=i, allow_slow_non_contiguous=True),
                       reads=reads, writes=writes)
        else:
            self.P.dma(q, lambda e, o=out_ap, i=in_ap: e.dma_start(out=o, in_=i), reads=reads, writes=writes)

    def wload(self, dst, src_ap):
        self.dma("pool", dst.ap, src_ap, reads=[], writes=[dst])

    def evac_engine(self):
        self._ev = getattr(self, "_ev", 0) + 1
        return "act" if self._ev % 2 else "dve"

    def X(self, blk, c0=0, n=1024):
        return self.f32(self.W_X + blk * 1024 + c0, n)

    def UT(self, k, t0, n, p0=0, p1=128):
        return self.bf(self.W_UT, k * 2048 + t0, n, p0, p1)

    def XB(self, j, c0=0, n=1024):
        return self.bf(self.W_XB + 512 * j, c0, n)

    def LNV(self, i, c0=0, n=1024):
        return self.f32(self.W_LNV + 1024 * i + c0, n)

    def PT(self, i, c0=0, n=512):
        return self.bf(self.W_PT + 256 * i, c0, n)

    def modcol(self, layer, vec, ch, seq):
        return self.f32(self.W_MODT + ((layer * 48) + vec * 8 + ch) * 2 + seq, 1)

    def IDB(self, n=128):
        return self.bf(self.W_IDB, 0, 128, 0, n) if n == 128 else V(
            self.arena[0:n, self.W_IDB:self.W_IDB + 64].bitcast(BF16)[:, 0:n], "sb",
            (self.W_IDB * 4) // GRAN, (self.W_IDB * 4) // GRAN + 1)

    def IDF(self, n):
        v = self.f32(self.W_IDF, 128, 0, n)
        return v.w(v.ap[:, 0:n])

    def prologue(self):
        dr = self.dr
        st0 = self.f32(self.tmp(), 512)
        self.dma("sp", st0.ap[:, 0:512], dr["cbf"][:, 0:512], reads=[], writes=[st0])
        for i, w in enumerate((self.W_IDB, self.W_ONB, self.W_NIB, self.W_UTB)):
            dst = self.bf(w, 0, 128)
            self.copy("dve", dst, st0.w(st0.ap[:, i * 128:(i + 1) * 128]))
        idf = self.f32(self.W_IDF, 128)
        self.dma("sp", idf.ap, dr["cbf"][:, 0:128], reads=[], writes=[idf])
        negh = self.f32(self.W_NEGH, 512)
        self.memset("pool", negh, -0.5)
        small = self.f32(self.W_GQ, 64 * 4 + 64)
        gq = self.f32(self.W_GQ, 2)
        self.dma("sp", gq.ap, dr["gq"][:, :], reads=[], writes=[gq])
        gkv = self.f32(self.W_GKV, 2)
        self.dma("sp", gkv.ap, dr["gkv"][:, :], reads=[], writes=[gkv])
        invf = self.f32(self.W_INVF, 2, 0, 64)
        self.dma("sp", invf.ap, dr["invf"][:, :], reads=[], writes=[invf])
        bfc = self.f32(self.W_BF, 1, 0, 16)
        self.dma("sp", bfc.ap, dr["fox_bf"][:, :], reads=[], writes=[bfc])
        nbf = self.f32(self.W_BF + 1, 1, 0, 16)
        self.ts("dve", nbf, bfc, -1.0, ALU.mult)
        wf = self.bf(self.W_WF, 0, 128)
        self.wload(wf.w(wf.ap.rearrange("p (k n) -> p k n", k=8)),
                   dr["fox_w_in"][:, 3072:3088].rearrange("(k p) n -> p k n", p=128))
        ct = self.f32(self.W_CT, 16)
        self.dma("sp", ct.ap, dr["cT"][:, :], reads=[], writes=[ct])
        cact = self.bf(self.W_CACT, 0, 16)
        self.act(cact, ct, AF.Silu)
        for layer in range(2):
            adab = self.f32(self.W_ADAB, 6144, 0, 2)
            self.dma("sp", adab.ap, dr["ada_b"][layer:layer + 1, :].to_broadcast([2, 6144]),
                     reads=[], writes=[adab])
            for cg in range(12):
                slot = cg % 2
                stg = self.f32(self.W_UT + 4096 * slot, 4096)
                self.dma("sp", stg.ap.rearrange("p (k n) -> p k n", k=8),
                         dr["ada_w"][layer, :, cg * 512:(cg + 1) * 512].rearrange("(k p) n -> p k n", p=128),
                         reads=[], writes=[stg])
                self.copy("act", self.bf(self.W_ADAW + 2048 * slot, 0, 2048), self.f32(self.W_UT + 4096 * slot, 2048))
                self.copy("dve", self.bf(self.W_ADAW + 2048 * slot, 2048, 2048), self.f32(self.W_UT + 4096 * slot + 2048, 2048))
                b = self.nbank()
                pso = self.ps(b, 0, 512, 0, 2)
                for k in range(8):
                    self.mm(pso, self.bf(self.W_CACT, 2 * k, 2),
                            self.bf(self.W_ADAW + 2048 * slot, k * 512, 512), k == 0, k == 7)
                mr = self.f32(self.W_MODROW + cg * 512, 512, 0, 2)
                self.tt("dve", mr, pso, self.f32(self.W_ADAB + cg * 512, 512, 0, 2), ALU.add)
            for v in (1, 2, 4, 5):
                sl = self.f32(self.W_MODROW + v * 1024, 1024, 0, 2)
                self.ts("dve", sl, sl, 1.0, ALU.add)
            b = self.nbank()
            for c in range(48):
                self.tr(self.ps(b, 2 * c, 2), self.f32(self.W_MODROW + c * 128, 128, 0, 2), self.IDF(2))
            self.copy("dve", self.f32(self.W_MODT + layer * 96, 96), self.ps(b, 0, 96))
            mrow = self.f32(self.W_MODROW, 6144, 0, 2)
            self.dma("sp", dr["modrows"][2 * layer:2 * layer + 2, :], mrow.ap,
                     reads=[mrow], writes=[V(None, "modrows", 2 * layer, 2 * layer + 2)])

    def load_x_block(self, seq, blk):
        xv = self.X(blk)
        r0 = seq * S + blk * 128
        self.dma("sp", xv.ap, self.dr["x"][r0:r0 + 128, :], reads=[], writes=[xv])

    def load_x(self, seq):
        for blk in range(NB):
            self.load_x_block(seq, blk)

    def seq_front(self, seq):
        self.load_x(seq)
        self.mla_load_weights()
        for st in range(4):
            for j in range(4):
                self.cast_xb(4 * st + j, j)
            self.ut_subtile(0, 0, 1, seq, st)
            self.mla_latents_rms(st)

    def cast_xb(self, blk, j):
        self.act(self.XB(j), self.X(blk), AF.Copy)

    def ut_subtile(self, layer, vec_sh, vec_sc, seq, st, act_only=False):
        for ch in range(8):
            b = self.nbank()
            for j in range(4):
                self.tr(self.psb(b, j * 128, 128), self.XB(j, ch * 128, 128), self.IDB())
            dst = self.UT(ch, st * 512, 512)
            src = self.psb(b, 0, 512)
            sc = self.modcol(layer, vec_sc, ch, seq)
            sh = self.modcol(layer, vec_sh, ch, seq)
            if ch % 2 == 0 or act_only:
                self.act(dst, src, AF.Identity, scale=sc, bias=sh)
            else:
                self.ts("dve", dst, src, sc, ALU.mult, sh, ALU.add)

    def build_ut_from_x(self, layer, vec_sh, vec_sc, seq):
        for st in range(4):
            for j in range(4):
                self.cast_xb(4 * st + j, j)
            self.ut_subtile(layer, vec_sh, vec_sc, seq, st)

    def load_lnv(self, layer, which, seq, gate_only=False, ln_only=False):
        dr = self.dr
        gvec = 2 if which == 0 else 5
        if not ln_only:
            g = self.LNV(0)
            row = 2 * layer + seq
            self.dma("sp", g.ap, dr["modrows"][row:row + 1, gvec * 1024:(gvec + 1) * 1024].to_broadcast([128, 1024]),
                     reads=[V(None, "modrows", row, row + 1)], writes=[g])
        if not gate_only:
            ga = self.LNV(1)
            self.dma("sp", ga.ap, dr["ln_g"][layer, which:which + 1, :].to_broadcast([128, 1024]), reads=[], writes=[ga])
            be = self.LNV(2)
            self.dma("sp", be.ap, dr["ln_b"][layer, which:which + 1, :].to_broadcast([128, 1024]), reads=[], writes=[be])

    def resid(self, first, blk, half, psv):
        t = self.f32(self.tmp(), 512)
        self.tt("dve", t, psv, self.LNV(0, half * 512, 512), ALU.mult)
        xv = self.X(blk, half * 512, 512)
        if first:
            self.stt(xv, xv, ALPHA, t, ALU.mult, ALU.add)
        else:
            self.tt("dve", xv, xv, t, ALU.add)

    def ln_stats(self, blk):
        sb = self.W_STAT2 + (blk % 2) * 128
        st6 = self.f32(sb, 12)
        for hf in range(2):
            o = self.f32(sb + 6 * hf, 6)
            xi = self.X(blk, hf * 512, 512)
            self.P.op("dve", lambda e, o=o.ap, i=xi.ap: e.bn_stats(out=o, in_=i), reads=[xi], writes=[o])
        mv = self.f32(sb + 12, 2)
        self.P.op("dve", lambda e, o=mv.ap, i=st6.ap: e.bn_aggr(out=o, in_=i), reads=[st6], writes=[mv])
        ve = self.f32(sb + 64, 1)
        rstd = self.f32(sb + 65, 1)
        self.ts("pool", ve, self.f32(sb + 13, 1), EPS, ALU.add)
        self.tt("pool", rstd, ve, self.f32(self.W_NEGH, 1), ALU.pow)
        xv = self.X(blk)
        self.stt(xv, xv, self.f32(sb + 12, 1), self.LNV(1), ALU.subtract, ALU.mult)

    def ln_finish(self, blk, final_seq=None):
        sb = self.W_STAT2 + (blk % 2) * 128
        xv = self.X(blk)
        self.stt(xv, xv, self.f32(sb + 65, 1), self.LNV(2), ALU.mult, ALU.add)
        if final_seq is not None:
            r0 = final_seq * S + blk * 128
            self.dma("sp", self.dr["out"][r0:r0 + 128, :], xv.ap, reads=[xv],
                     writes=[V(None, "out", final_seq * NB + blk, final_seq * NB + blk + 1)])

    def ln_pipe(self, layer, which, seq, next_mod=None, final=False, hook=None, next_seq=None):
        pending = []
        if final and next_seq is not None:
            next_mod = (0, 0, 1)
            hook = lambda st: [lambda: self.rms_latent(st, (0, 1), self.W_WIN_M, self.W_GQ, self.W_CQ),
                               lambda: self.rms_latent(st, (2, 3), self.W_WIN_M, self.W_GKV, self.W_CKV)]
        ut_seq = seq if next_seq is None else next_seq

        def step(t):
            if t < NB:
                self.ln_stats(t)
                self.ln_finish(t, final_seq=(seq if final else None))
                if final and next_seq is not None:
                    self.load_x_block(next_seq, t)
            b = t - 1
            if 0 <= b < NB and (not final or next_seq is not None):
                self.cast_xb(b, b % 4)
                if b % 4 == 3:
                    nl, vsh, vsc = next_mod
                    self.ut_subtile(nl, vsh, vsc, ut_seq, b // 4, act_only=True)
                    if hook is not None:
                        pending.extend(hook(b // 4))
            if pending:
                pending.pop(0)()

        def flush():
            step(NB)
            step(NB + 1)
            for fn in pending:
                fn()
            del pending[:]

        return step, flush

    def ln_stage(self, layer, which, seq, **kw):
        step, flush = self.ln_pipe(layer, which, seq, **kw)
        for t in range(NB):
            step(t)
        flush()

    def range_wrap(self, out, in_, shift, scratch):
        if shift != 0.0:
            self.ts("dve", out, in_, shift, ALU.add)
            src = out
        else:
            src = in_
        self.ts("dve", scratch, src, math.pi, ALU.is_gt)
        self.stt(out, scratch, -TWO_PI, src, ALU.mult, ALU.add)

    def rope_tables(self, seq):
        dr = self.dr
        posi = self.i32(self.W_TMP, 2048, 0, 64)
        self.dma("sp", posi.ap, dr["pos"][seq:seq + 1, :].to_broadcast([64, 2048]), reads=[], writes=[posi])
        cosv = self.f32(self.W_TAB, 2048, 0, 64)
        sinv = self.f32(self.W_TAB + 2048, 2048, 0, 64)
        ang = cosv
        self.copy("dve", ang, posi)
        invf = self.f32(self.W_INVF, 1, 0, 64)
        sgn = self.f32(self.W_INVF + 1, 1, 0, 64)
        self.ts("dve", ang, ang, invf, ALU.mult)
        kf = sinv
        self.ts("dve", kf, ang, 1.0 / TWO_PI, ALU.mult)
        ki = self.i32(self.W_TMP, 2048, 0, 64)
        self.copy("dve", ki, kf)
        self.copy("dve", kf, ki)
        self.stt(ang, kf, -CW1, ang, ALU.mult, ALU.add)
        self.stt(ang, kf, -CW2, ang, ALU.mult, ALU.add)
        m1 = self.f32(self.W_TMP, 2048, 0, 64)
        self.range_wrap(m1, ang, 0.0, kf)
        self.act(sinv, m1, AF.Sin, scale=sgn)
        m2 = self.f32(self.W_TMP, 2048, 0, 64)
        self.range_wrap(m2, m1, math.pi / 2, ang)
        self.act(cosv, m2, AF.Sin)

    def mla_load_weights(self):
        dr = self.dr
        w = self.bf(self.W_WIN_M, 0, 8 * 640)
        self.wload(w.w(w.ap.rearrange("p (k n) -> p k n", k=8)),
                   dr["mla_w_in"].rearrange("(k p) n -> p k n", p=128))
        w = self.bf(self.W_WUQ, 0, 2 * 2048)
        self.wload(w.w(w.ap.rearrange("p (k n) -> p k n", k=2)),
                   dr["mla_w_uq"].rearrange("(k p) n -> p k n", p=128))
        w = self.bf(self.W_WUK, 0, 2 * 1024)
        self.wload(w.w(w.ap.rearrange("p (k n) -> p k n", k=2)),
                   dr["mla_w_uk"].rearrange("(k p) n -> p k n", p=128))
        w = self.bf(self.W_WUV, 0, 2 * 1024)
        self.wload(w.w(w.ap.rearrange("p (k n) -> p k n", k=2)),
                   dr["mla_w_uv"].rearrange("(k p) n -> p k n", p=128))

    def rms_latent(self, st, oc_pair, wbase, gcol_w, dst_w):
        hraw = []
        sqs = []
        for i, oc in enumerate(oc_pair):
            b = self.nbank()
            pso = self.ps(b)
            for k in range(8):
                self.mm(pso, self.bf(wbase, k * 640 + oc * 128, 128), self.UT(k, st * 512, 512), k == 0, k == 7)
            sq = self.bf(self.tmp(), 0, 512)
            self.act(sq, pso, AF.Square)
            hr = self.f32(self.tmp(), 512)
            self.copy("dve", hr, pso)
            hraw.append(hr)
            sqs.append(sq)
        b = self.nbank()
        ssq = self.ps(b)
        self.mm(ssq, self.bf(self.W_ONB, 0, 128), sqs[0], True, False)
        self.mm(ssq, self.bf(self.W_ONB, 0, 128), sqs[1], False, True)
        ms = self.f32(self.tmp(), 512)
        self.act(ms, ssq, AF.Sqrt, scale=1.0 / 256.0, bias=EPS)
        rstd = self.f32(self.tmp(), 512)
        self.P.op("dve", lambda e, o=rstd.ap, a=ms.ap: e.reciprocal(out=o, in_=a), reads=[ms], writes=[rstd])
        for i in range(2):
            dst = self.bf(dst_w, i * 2048 + st * 512, 512)
            self.stt(dst, hraw[i], self.f32(gcol_w + i, 1), rstd, ALU.mult, ALU.mult)

    def mla_latents_rms(self, st):
        self.rms_latent(st, (0, 1), self.W_WIN_M, self.W_GQ, self.W_CQ)
        self.rms_latent(st, (2, 3), self.W_WIN_M, self.W_GKV, self.W_CKV)

    def mla_latents_kr(self, st):
        ba = self.nbank()
        bb = self.nbank()
        pa = self.ps(ba, 0, 512, 0, 64)
        pb = self.ps(bb, 0, 512, 0, 64)
        for k in range(8):
            self.mm(pa, self.bf(self.W_WIN_M, k * 640 + 512, 64), self.UT(k, st * 512, 512), k == 0, k == 7)
        for k in range(8):
            self.mm(pb, self.bf(self.W_WIN_M, k * 640 + 576, 64), self.UT(k, st * 512, 512), k == 0, k == 7)
        self.rope_combine(pa, pb, st, self.bf(self.W_KR, st * 512, 512, 0, 64))

    def rope_combine(self, pa, pb, st, dst):
        t1 = self.f32(self.tmp(), 512, 0, 64)
        t2 = self.f32(self.tmp(), 512, 0, 64)
        self.tt("dve", t1, pa, self.f32(self.W_TAB + st * 512, 512, 0, 64), ALU.mult)
        self.tt("dve", t2, pb, self.f32(self.W_TAB + 2048 + st * 512, 512, 0, 64), ALU.mult)
        self.tt("dve", dst, t1, t2, ALU.add)

    def mla_head_proj(self, h):
        for st in range(4):
            t0 = st * 512
            b = self.nbank()
            pso = self.ps(b)
            for k in range(2):
                self.mm(pso, self.bf(self.W_WUQ, k * 2048 + h * 256, 128), self.bf(self.W_CQ, k * 2048 + t0, 512), k == 0, k == 1)
            self.copy("act", self.bf(self.W_QN, t0, 512), pso)
            pa = self.ps(self.nbank(), 0, 512, 0, 64)
            for k in range(2):
                self.mm(pa, self.bf(self.W_WUQ, k * 2048 + h * 256 + 128, 64), self.bf(self.W_CQ, k * 2048 + t0, 512), k == 0, k == 1)
            t1 = self.f32(self.tmp(), 512, 0, 64)
            self.tt("dve", t1, pa, self.f32(self.W_TAB + t0, 512, 0, 64), ALU.mult)
            pb = self.ps(self.nbank(), 0, 512, 0, 64)
            for k in range(2):
                self.mm(pb, self.bf(self.W_WUQ, k * 2048 + h * 256 + 192, 64), self.bf(self.W_CQ, k * 2048 + t0, 512), k == 0, k == 1)
            t2 = self.f32(self.tmp(), 512, 0, 64)
            self.tt("dve", t2, pb, self.f32(self.W_TAB + 2048 + t0, 512, 0, 64), ALU.mult)
            self.tt("dve", self.bf(self.W_QR, t0, 512, 0, 64), t1, t2, ALU.add)
            b = self.nbank()
            pso = self.ps(b)
            for k in range(2):
                self.mm(pso, self.bf(self.W_WUK, k * 1024 + h * 128, 128), self.bf(self.W_CKV, k * 2048 + t0, 512), k == 0, k == 1)
            self.copy("dve", self.bf(self.W_KN, t0, 512), pso)
        for bq in range(4):
            b = self.nbank()
            for j in range(4):
                blk = bq * 4 + j
                pso = self.ps(b, j * 128, 128)
                for k in range(2):
                    self.mm(pso, self.bf(self.W_CKV, k * 2048 + blk * 128, 128), self.bf(self.W_WUV, k * 1024 + h * 128, 128), k == 0, k == 1)
            self.copy("act", self.bf(self.W_VH, bq * 512, 512), self.ps(b))

    def attn_core(self, s_fn, pv_fn, fin_fn, sbanks):
        la = len(sbanks) - 1
        tiles = []
        for i in range(4):
            nj = 4 * i + 4
            for j in range(nj):
                c0 = (j - 4 * i) * 128 if j >= 4 * i else 0
                tiles.append((i, j, c0, nj))
        n = len(tiles)
        for t in range(n + la):
            if t < n:
                i, j, c0, nj = tiles[t]
                s_fn(i, j, c0, sbanks[t % len(sbanks)], t % 4)
            u = t - la
            if u >= 0:
                i, j, c0, nj = tiles[u]
                pv_fn(i, j, c0, u % 4, nj)
                if j == nj - 1:
                    fin_fn(i)

    def mask_mm(self, S_diag):
        self.mm(S_diag, self.bf(self.W_NIB, 0, 128), self.bf(self.W_UTB, 0, 128), False, True)

    def mla_head_attn(self, h, ao_chunk):
        def s_fn(i, j, c0, sb, pt):
            n = 512 - c0
            S_ = self.ps(sb, c0, n)
            diag = j >= 4 * i
            self.mm(S_, self.bf(self.W_KN, j * 128, 128), self.bf(self.W_QN, i * 512 + c0, n), True, False)
            self.mm(S_, self.bf(self.W_KR, j * 128, 128, 0, 64), self.bf(self.W_QR, i * 512 + c0, n, 0, 64), False, not diag)
            if diag:
                self.mask_mm(self.ps(sb, c0, 128))
            P_ = self.PT(pt, c0, n)
            self.act(P_, S_, AF.Exp, scale=MLA_SCALE)

        def pv_fn(i, j, c0, pt, nj):
            n = 512 - c0
            P_ = self.PT(pt, c0, n)
            self.mm(self.ps(2 + i % 2, c0, n), self.bf(self.W_VH, j * 128, 128), P_, j == 0, j == nj - 1)
            self.mm(self.ps(4 + i % 2, c0, n), self.bf(self.W_ONB, 0, 128), P_, j == 0, j == nj - 1)

        def fin_fn(i):
            r = self.f32(self.tmp(), 512)
            self.P.op("dve", lambda e, o=r.ap, a=self.ps(4 + i % 2).ap: e.reciprocal(out=o, in_=a),
                      reads=[self.ps(4 + i % 2)], writes=[r])
            self.tt("dve", self.UT(ao_chunk, i * 512, 512), self.ps(2 + i % 2), r, ALU.mult)

        self.attn_core(s_fn, pv_fn, fin_fn, (0, 1, 6))

    def mla_mixer(self, seq, tail_ln=None):
        dr = self.dr
        self.rope_tables(seq)
        self.cut("p2")
        for st in range(4):
            self.mla_latents_kr(st)
        self.cut("p3")
        self.load_lnv(0, 0, seq, gate_only=True)
        for g in range(2):
            wo = self.bf(self.W_WO_M, 0, 4 * 1024)
            self.wload(wo.w(wo.ap.rearrange("p (k n) -> p k n", k=4)),
                       dr["mla_w_o"][g * 512:(g + 1) * 512, :].rearrange("(k p) n -> p k n", p=128))
            for hh in range(4):
                h = 4 * g + hh
                self.mla_head_proj(h)
                self.cut("p4")
                self.mla_head_attn(h, 4 * g + hh)
                self.cut("p5")
            tail = None
            if g == 1 and tail_ln is not None:
                tail = tail_ln()
            for blk in range(NB):
                for half in range(2):
                    b = self.nbank()
                    pso = self.ps(b)
                    for hh in range(4):
                        self.mm(pso, self.UT(4 * g + hh, blk * 128, 128), self.bf(self.W_WO_M, hh * 1024 + half * 512, 512), hh == 0, hh == 3)
                    self.resid(g == 0, blk, half, pso)
                if tail is not None:
                    tail[0](blk)
            if tail is not None:
                tail[1]()

    def ffn_load_gu(self, layer, cg):
        dr = self.dr
        slot = cg % 2
        wg = self.bf(self.W_WG + 1024 * slot, 0, 2048)
        wu = self.bf(self.W_WU + 1024 * slot, 0, 2048)
        self.wload(wg.w(wg.ap.rearrange("p (k n) -> p k n", k=8)),
                   dr["ffn_w_gate"][layer, :, cg * 256:(cg + 1) * 256].rearrange("(k p) n -> p k n", p=128))
        self.wload(wu.w(wu.ap.rearrange("p (k n) -> p k n", k=8)),
                   dr["ffn_w_up"][layer, :, cg * 256:(cg + 1) * 256].rearrange("(k p) n -> p k n", p=128))

    def ffn_gu(self, cg, jj, st, j0):
        slot = cg % 2
        j = 2 * cg + jj
        bg = self.nbank()
        bu = self.nbank()
        pg = self.ps(bg)
        pu = self.ps(bu)
        for k in range(8):
            self.mm(pg, self.bf(self.W_WG + 1024 * slot, k * 256 + jj * 128, 128), self.UT(k, st * 512, 512), k == 0, k == 7)
        for k in range(8):
            self.mm(pu, self.bf(self.W_WU + 1024 * slot, k * 256 + jj * 128, 128), self.UT(k, st * 512, 512), k == 0, k == 7)
        t = self.f32(self.tmp(), 512)
        self.act(t, pg, AF.Silu)
        self.tt("dve", self.bf(self.W_HT + 1024 * self.ht_slot(j, j0), st * 512, 512), t, pu, ALU.mult)

    @staticmethod
    def ht_slot(j, j0):
        return (j - j0 + 4) % 8 if j0 == 0 else j - j0

    def ffn_hook(self, layer):
        def hook(st):
            return [lambda cg=cg, jj=jj: self.ffn_gu(cg, jj, st, 0) for cg in (0, 1) for jj in range(2)]
        return hook

    def ffn(self, layer, seq, pre=0, tail_ln=None):
        dr = self.dr
        wd_d = dr["ffn_w_down"]
        groups = [(0, 8), (8, 16), (16, 22)]
        dcnt = 0
        for gi, (j0, j1) in enumerate(groups):
            for cg in range(j0 // 2, j1 // 2):
                if cg < pre:
                    continue
                self.ffn_load_gu(layer, cg)
                for jj in range(2):
                    for st in range(4):
                        self.ffn_gu(cg, jj, st, j0)
            nch = j1 - j0
            last = gi == len(groups) - 1

            def load_wd(half, slot):
                wd = self.bf(self.W_WD + 2048 * slot, 0, nch * 512)
                self.wload(wd.w(wd.ap.rearrange("p (k n) -> p k n", k=nch)),
                           wd_d[layer, j0 * 128:j1 * 128, half * 512:(half + 1) * 512].rearrange("(k p) n -> p k n", p=128))

            def down(blk, half, slot):
                b = self.nbank()
                pso = self.ps(b)
                for jl in range(nch):
                    self.mm(pso, self.bf(self.W_HT + 1024 * self.ht_slot(j0 + jl, j0), blk * 128, 128),
                            self.bf(self.W_WD + 2048 * slot, jl * 512, 512), jl == 0, jl == nch - 1)
                self.resid(gi == 0, blk, half, pso)

            if last and tail_ln is not None:
                load_wd(0, dcnt % 2)
                load_wd(1, (dcnt + 1) % 2)
                step, flush = tail_ln
                for blk in range(NB):
                    down(blk, 0, dcnt % 2)
                    down(blk, 1, (dcnt + 1) % 2)
                    step(blk)
                flush()
                dcnt += 2
            else:
                for half in range(2):
                    slot = dcnt % 2
                    dcnt += 1
                    load_wd(half, slot)
                    for blk in range(NB):
                        down(blk, half, slot)

    def fox_pro_subtile(self, st):
        t0 = st * 512
        b = self.nbank()
        pso = self.ps(b, 0, 512, 0, 16)
        for k in range(8):
            self.mm(pso, self.bf(self.W_WF, k * 16, 16), self.UT(k, t0, 512), k == 0, k == 7)
        e_ = self.f32(self.tmp(), 512, 0, 16)
        self.act(e_, pso, AF.Exp, scale=-1.0, bias=self.f32(self.W_BF + 1, 1, 0, 16))
        self.act(e_, e_, AF.Ln, scale=1.0, bias=1.0)
        g2w = self.W_GC + 512 * (st % 2)
        g2 = self.f32(g2w, 512, 0, 16)
        if st == 0:
            self.P.op("dve", lambda e, o=g2.ap, a=e_.ap: e.tensor_tensor_scan(o, a, a, 0.0, ALU.add, ALU.max),
                      reads=[e_], writes=[g2])
        else:
            ini = self.f32(self.W_GC + 512 * ((st - 1) % 2) + 511, 1, 0, 16)
            self.P.op("dve", lambda e, o=g2.ap, a=e_.ap, i=ini.ap: e.tensor_tensor_scan(o, a, a, i, ALU.add, ALU.max),
                      reads=[e_, ini], writes=[g2])
        ft = self.W_FT
        hi = self.bf(self.W_FSP, t0, 512, 0, 16)
        self.ts("dve", hi, g2, -8.0, ALU.mult)
        r1 = e_
        self.stt(r1, g2, -8.0, hi, ALU.mult, ALU.subtract)
        mid = self.bf(ft, 0, 512, 0, 16)
        self.copy("dve", mid, r1)
        self.tt("dve", r1, r1, mid, ALU.subtract)
        lo = self.bf(ft + 256, 0, 512, 0, 16)
        self.copy("dve", lo, r1)
        self.copy("pool", self.bf(self.W_FSP, t0, 512, 32, 48), mid)
        self.copy("pool", self.bf(self.W_FSP, t0, 512, 64, 80), lo)

    def fox_setup(self):
        for ab in range(2):
            self.memset("pool", self.bf(self.W_KA + 1024 * ab, 0, 2048, 64, 70), 1.0)
            self.memset("pool", self.bf(self.W_QA + 1024 * ab, 0, 2048, 64, 70), -1.0)
        vg = self.bf(self.W_VG, 0, 8192)
        v5 = vg.ap.rearrange("p (b q t e) -> p b q t e", b=16, q=2, t=2)
        self.memset("pool", vg.w(v5[:, :, :, 0, 64:128]), 1.0)
        self.memset("pool", vg.w(v5[:, :, :, 1, 0:64]), 1.0)

    def fox_load_group(self, g):
        dr = self.dr
        win = dr["fox_w_in"]
        for part in range(3):
            w = self.bf(self.W_WIN_F, 0, 8 * 768)
            w3 = w.ap.rearrange("p (k n) -> p k n", k=8)
            c0 = part * 1024 + g * 256
            self.wload(w.w(w3[:, :, part * 256:(part + 1) * 256]),
                       win[:, c0:c0 + 256].rearrange("(k p) n -> p k n", p=128))
        wo = self.bf(self.W_WO_F, 0, 2048)
        self.wload(wo.w(wo.ap.rearrange("p (k n) -> p k n", k=2)),
                   dr["fox_w_o"][g * 256:(g + 1) * 256, :].rearrange("(k p) n -> p k n", p=128))

    def fox_vproj_unit(self, b2):
        vg = self.bf(self.W_VG, 0, 8192)
        v5 = vg.ap.rearrange("p (b q t e) -> p b q t e", b=16, q=2, t=2)
        b = self.nbank()
        for jb in range(2):
            blk = 2 * b2 + jb
            pso = self.ps(b, jb * 256, 256)
            for k in range(8):
                self.mm(pso, self.UT(k, blk * 128, 128), self.bf(self.W_WIN_F, k * 768 + 512, 256), k == 0, k == 7)
        psv = self.ps(b)
        p5 = psv.ap.rearrange("p (b q t d) -> p b q t d", b=2, q=2, t=2)
        self.copy("dve", vg.w(v5[:, 2 * b2:2 * b2 + 2, :, 0, 0:64]), psv.w(p5[:, :, :, 0, :]))
        self.copy("dve", vg.w(v5[:, 2 * b2:2 * b2 + 2, :, 1, 64:128]), psv.w(p5[:, :, :, 1, :]))

    def fox_qk_proj(self, pair, st):
        t0 = st * 512
        for which, base_w, col0 in ((0, self.W_QA, 0), (1, self.W_KA, 256)):
            b = self.nbank()
            pp = self.ps(b)
            for k in range(8):
                self.mm(pp, self.bf(self.W_WIN_F, k * 768 + col0 + pair * 128, 128), self.UT(k, t0, 512), k == 0, k == 7)
            self.copy("dve", self.bf(base_w, t0, 512, 0, 64), self.ps(b, 0, 512, 0, 64))
            self.copy("dve", self.bf(base_w + 1024, t0, 512, 0, 64), self.ps(b, 0, 512, 64, 128))

    def fox_hook(self):
        def hook(st):
            return [lambda: self.fox_pro_subtile(st),
                    lambda: self.fox_vproj_unit(2 * st),
                    lambda: self.fox_vproj_unit(2 * st + 1),
                    lambda: self.fox_qk_proj(0, st)]
        return hook

    def fox_group(self, seq, g, pre=False, tail_ln=None):
        if not pre:
            self.fox_load_group(g)
            for b2 in range(8):
                self.fox_vproj_unit(b2)
        for pair in range(2):
            if not (pre and pair == 0):
                for st in range(4):
                    self.fox_qk_proj(pair, st)
            for ab in range(2):
                hh = 2 * pair + ab
                h = 4 * g + hh
                a = self.W_FSP
                src = self.arena[h:h + 65:32, a:a + 1024].bitcast(BF16)
                qa_aug = self.bf(self.W_QA + 1024 * ab, 0, 2048, 64, 67)
                self.dma("sp", qa_aug.ap, src, reads=[self.bf(self.W_FSP, 0, 2048, 0, 80)], writes=[qa_aug])
                ka_aug = self.bf(self.W_KA + 1024 * ab, 0, 2048, 67, 70)
                self.dma("sp", ka_aug.ap, src, reads=[self.bf(self.W_FSP, 0, 2048, 0, 80)], writes=[ka_aug])
            for ab in range(2):
                hh = 2 * pair + ab
                self.fox_head_attn(4 * g + hh, hh, g, ab)
        aob = self.W_AO
        tail = None
        if g == 3 and tail_ln is not None:
            tail = tail_ln()
        for blk in range(NB):
            for half in range(2):
                b = self.nbank()
                pso = self.ps(b)
                for c in range(2):
                    self.mm(pso, self.bf(aob, c * 2048 + blk * 128, 128), self.bf(self.W_WO_F, c * 1024 + half * 512, 512), c == 0, c == 1)
                self.resid(g == 0, blk, half, pso)
            if tail is not None:
                tail[0](blk)
        if tail is not None:
            tail[1]()

    def fox_head_attn(self, h, hh, g, ab):
        vg = self.bf(self.W_VG, 0, 8192)
        v5 = vg.ap.rearrange("p (b q e) -> p b q e", b=16, q=4)
        aob = self.W_AO
        odd = hh % 2
        qaw = self.W_QA + 1024 * ab
        kaw = self.W_KA + 1024 * ab

        def s_fn(i, j, c0, sb, pt):
            n = 512 - c0
            S_ = self.ps(sb, c0, n)
            diag = j >= 4 * i
            self.mm(S_, self.bf(kaw, j * 128, 128, 0, 70), self.bf(qaw, i * 512 + c0, n, 0, 70), True, not diag)
            if diag:
                self.mask_mm(self.ps(sb, c0, 128))
            P_ = self.PT(pt, c0, n)
            self.act(P_, S_, AF.Exp, scale=0.125)

        def pv_fn(i, j, c0, pt, nj):
            n = 512 - c0
            P_ = self.PT(pt, c0, n)
            lhs = vg.w(v5[:, j, hh, :])
            self.mm(self.ps(2 + i % 2, c0, n), lhs, P_, j == 0, j == nj - 1)

        def fin_fn(i):
            ob = 2 + i % 2
            tw = self.tmp()
            if not odd:
                r = self.f32(tw, 512, 0, 64)
                lsrc = self.ps(ob, 0, 512, 64, 128)
                osrc = self.ps(ob, 0, 512, 0, 64)
                dst = self.bf(aob, (hh // 2) * 2048 + i * 512, 512, 0, 64)
            else:
                r = self.f32(tw, 512, 64, 128)
                lsrc = self.ps(ob, 0, 512, 0, 64)
                osrc = self.ps(ob, 0, 512, 64, 128)
                dst = self.bf(aob, (hh // 2) * 2048 + i * 512, 512, 64, 128)
            self.P.op("dve", lambda e, o=r.ap, a=lsrc.ap: e.reciprocal(out=o, in_=a), reads=[lsrc], writes=[r])
            self.tt("dve", dst, osrc, r, ALU.mult)

        self.attn_core(s_fn, pv_fn, fin_fn, (0, 1, 4))

    def fox_mixer(self, seq, tail_ln=None):
        self.load_lnv(1, 0, seq, gate_only=True)
        for g in range(4):
            self.fox_group(seq, g, pre=(g == 0), tail_ln=tail_ln)

    def cut(self, name):
        if self.stop_after == name:
            raise StopBuild()

    def forward(self):
        try:
            self._forward()
        except StopBuild:
            pass
        self.P.finish()

    def _forward(self):
        self.prologue()
        self.cut("p0")
        front_done = False
        for seq in range(self.nseq):
            if not front_done:
                self.seq_front(seq)
            front_done = False
            self.cut("p1")
            if self.stop_after == "l0a":
                self.mla_mixer(seq)
                self.load_lnv(0, 0, seq, ln_only=True)
                self.ln_stage(0, 0, seq, final=True)
                continue

            def tail_l0a(seq=seq):
                self.load_lnv(0, 0, seq, ln_only=True)
                self.ffn_load_gu(0, 0)
                self.ffn_load_gu(0, 1)
                return self.ln_pipe(0, 0, seq, next_mod=(0, 3, 4), hook=self.ffn_hook(0))
            self.mla_mixer(seq, tail_ln=tail_l0a)
            self.load_lnv(0, 1, seq, gate_only=True)
            self.ffn(0, seq, pre=2)
            self.load_lnv(0, 1, seq, ln_only=True)
            if self.stop_after == "l0f":
                self.ln_stage(0, 1, seq, final=True)
                continue
            self.fox_setup()
            self.fox_load_group(0)
            self.ln_stage(0, 1, seq, next_mod=(1, 0, 1), hook=self.fox_hook())
            if self.stop_after == "l1a":
                self.fox_mixer(seq)
                self.load_lnv(1, 0, seq, ln_only=True)
                self.ln_stage(1, 0, seq, final=True)
                continue

            def tail_l1a(seq=seq):
                self.load_lnv(1, 0, seq, ln_only=True)
                self.ffn_load_gu(1, 0)
                self.ffn_load_gu(1, 1)
                return self.ln_pipe(1, 0, seq, next_mod=(1, 3, 4), hook=self.ffn_hook(1))
            self.fox_mixer(seq, tail_ln=tail_l1a)
            self.load_lnv(1, 1, seq, gate_only=True)
            if seq + 1 < self.nseq:
                self.ffn(1, seq, pre=2)
                self.load_lnv(1, 1, seq, ln_only=True)
                self.mla_load_weights()
                self.ln_stage(1, 1, seq, final=True, next_seq=seq + 1)
                front_done = True
            else:
                self.load_lnv(1, 1, seq, ln_only=True)
                self.ffn(1, seq, pre=2, tail_ln=self.ln_pipe(1, 1, seq, final=True))


def build_program(stop_after=None, nseq=NSEQ):
    nc = bass.Bass("TRN2", target_bir_lowering=False)
    dram = {}

    def din(name, shape, dt=F32):
        dram[name] = nc.dram_tensor(name, list(shape), dt, kind="ExternalInput").ap()

    din("x", [NSEQ * S, D])
    din("cT", [128, 16])
    din("pos", [NSEQ, S], I32)
    din("cbf", [128, 512])
    din("gq", [128, 2])
    din("gkv", [128, 2])
    din("invf", [64, 2])
    din("fox_bf", [16, 1])
    din("mla_w_in", [D, 640])
    din("mla_w_uq", [256, 2048])
    din("mla_w_uk", [256, 1024])
    din("mla_w_uv", [256, 1024])
    din("mla_w_o", [D, D])
    din("fox_w_in", [D, 3088])
    din("fox_w_o", [D, D])
    din("ada_w", [2, D, 6 * D])
    din("ada_b", [2, 6 * D])
    din("ffn_w_gate", [2, D, DFF])
    din("ffn_w_up", [2, D, DFF])
    din("ffn_w_down", [2, DFF, D])
    din("ln_g", [2, 2, D])
    din("ln_b", [2, 2, D])
    dram["out"] = nc.dram_tensor("out", [NSEQ * S, D], F32, kind="ExternalOutput").ap()
    dram["modrows"] = nc.dram_tensor("modrows", [4, 6 * D], F32).ap()

    with ExitStack() as es:
        arena = es.enter_context(nc.sbuf_tensor("arena", [128, SB_WORDS], F32))
        psum = es.enter_context(nc.psum_tensor("psum", [128, 4096], F32))
        bld = Builder(nc, arena, psum, dram, stop_after=stop_after, nseq=nseq)
        bld.forward()
        P = bld.P
        sems = {}
        for n, E in P.eng.items():
            sems[E.key] = es.enter_context(nc.semaphore("s_" + n))
        for q in ("sp", "pool", "act"):
            for slot in range(NDMA_SEMS):
                if P.eng[q].dma_vals[slot] > 0:
                    key = "D:%s:%d" % (q, slot)
                    sems[key] = es.enter_context(nc.semaphore("d_%s_%d" % (q, slot)))
        block = es.enter_context(nc.Block())

        def emit(e, E):
            own = sems[E.key]
            for it in E.items:
                for key, val in it.waits:
                    e.wait_ge(sems[key], val)
                if it.fn is None:
                    continue
                ins = it.fn(e)
                if it.dma is not None:
                    ins.then_inc(sems[it.dma[0]], 16)
                elif it.signal:
                    ins.then_inc(own, 1)

        @block.tensor
        def _(e):
            emit(e, P.eng["pe"])

        @block.scalar
        def _(e):
            emit(e, P.eng["act"])

        @block.vector
        def _(e):
            emit(e, P.eng["dve"])

        @block.gpsimd
        def _(e):
            emit(e, P.eng["pool"])

        @block.sync
        def _(e):
            emit(e, P.eng["sp"])
    return nc, P


def host_inputs(inputs):
    f = np.float32
    x = np.ascontiguousarray(inputs["x"], dtype=f)
    c = np.asarray(inputs["c"], dtype=f)
    pos = np.ascontiguousarray(inputs["positions"]).astype(np.int32)
    w_in = np.asarray(inputs["mla_w_in"][0], dtype=f)
    kr = w_in[:, 512:576]
    kr_sw = np.concatenate([kr[:, 32:], kr[:, :32]], axis=1)
    w_in_ext = np.ascontiguousarray(np.concatenate([w_in[:, :576], kr_sw], axis=1))
    w_uq = np.asarray(inputs["mla_w_uq"][0], dtype=f).reshape(256, 8, 192)
    rope = w_uq[:, :, 128:192]
    rope_sw = np.concatenate([rope[:, :, 32:], rope[:, :, :32]], axis=2)
    w_uq_ext = np.ascontiguousarray(np.concatenate([w_uq, rope_sw], axis=2).reshape(256, 2048))
    ident = np.eye(128, dtype=f)
    ones = np.ones((128, 128), dtype=f)
    negi = (-30000.0 * ident).astype(f)
    utri = (np.arange(128)[:, None] > np.arange(128)[None, :]).astype(f)
    cbf = np.ascontiguousarray(np.concatenate([ident, ones, negi, utri], axis=1))
    half = 32
    inv = (10000.0 ** (-np.arange(half, dtype=np.float32) / half)).astype(f)
    invf = np.zeros((64, 2), dtype=f)
    invf[:, 0] = np.concatenate([inv, inv])
    invf[:32, 1] = -1.0
    invf[32:, 1] = 1.0
    shared = {
        "cbf": cbf,
        "gq": np.ascontiguousarray(np.asarray(inputs["mla_g_q"][0], dtype=f).reshape(2, 128).T),
        "gkv": np.ascontiguousarray(np.asarray(inputs["mla_g_kv"][0], dtype=f).reshape(2, 128).T),
        "invf": invf,
        "fox_bf": np.ascontiguousarray(np.asarray(inputs["fox_b_f"][0], dtype=f).reshape(16, 1)),
        "mla_w_in": w_in_ext,
        "mla_w_uq": w_uq_ext,
        "mla_w_uk": np.ascontiguousarray(inputs["mla_w_uk"][0], dtype=f),
        "mla_w_uv": np.ascontiguousarray(inputs["mla_w_uv"][0], dtype=f),
        "mla_w_o": np.ascontiguousarray(inputs["mla_w_o"][0], dtype=f),
        "fox_w_in": np.ascontiguousarray(inputs["fox_w_in"][0], dtype=f),
        "fox_w_o": np.ascontiguousarray(inputs["fox_w_o"][0], dtype=f),
        "ada_w": np.ascontiguousarray(inputs["ada_w"], dtype=f),
        "ada_b": np.ascontiguousarray(inputs["ada_b"], dtype=f),
        "ffn_w_gate": np.ascontiguousarray(inputs["ffn_w_gate"], dtype=f),
        "ffn_w_up": np.ascontiguousarray(inputs["ffn_w_up"], dtype=f),
        "ffn_w_down": np.ascontiguousarray(inputs["ffn_w_down"], dtype=f),
        "ln_g": np.ascontiguousarray(inputs["ln_g"], dtype=f),
        "ln_b": np.ascontiguousarray(inputs["ln_b"], dtype=f),
    }
    in_maps = []
    for core in range(NCORES):
        b0 = core * NSEQ
        m = dict(shared)
        m["x"] = np.ascontiguousarray(x[b0:b0 + NSEQ].reshape(NSEQ * S, D))
        cc = c[b0:b0 + NSEQ]
        m["cT"] = np.ascontiguousarray(cc.reshape(NSEQ, 8, 128).transpose(2, 1, 0).reshape(128, 16))
        m["pos"] = np.ascontiguousarray(pos[b0:b0 + NSEQ])
        in_maps.append(m)
    return in_maps


def kernel(**inputs):
    in_maps = host_inputs(inputs)
    nc, _ = build_program()
    res = run_bass_kernel_spmd(nc, in_maps, core_ids=list(range(NCORES)))
    outs = [np.asarray(r["out"]).reshape(NSEQ, S, D) for r in res.results]
    return np.concatenate(outs, axis=0).astype(np.float32)
```
